# Optimizing a Trainium2 kernel written in Bass

```python
import math
import jax, jax.numpy as jnp
from jax import lax
import numpy as np

D_MODEL = 4096
BATCH = 2
SEQ = 4096
DEPTH = 1
DEC_BATCH = 16
DEC_SEQ = 16
PAST_LEN = 4096

CHUNK = 64
N_HEADS = 16
HEAD_DIM = D_MODEL // (2 * N_HEADS)
QK_WIDTH = N_HEADS * 2 * HEAD_DIM
V_WIDTH = N_HEADS * 2 * HEAD_DIM
SGU_CHUNK = 128
SGU_GROUPS = 8
SGU_WIDTH = D_MODEL
SGU_GROUP_DIM = SGU_WIDTH // SGU_GROUPS
D_FF = 4 * D_MODEL
ROPE_THETA = 10000.0
EPS = 1e-6
Q_BLOCK = 128
NEG_INF = -1e30
IN_WIDTH = 2 * QK_WIDTH + V_WIDTH + 2 * SGU_WIDTH + 2 * D_MODEL
SPLITS = (QK_WIDTH, 2 * QK_WIDTH, 2 * QK_WIDTH + V_WIDTH,
          2 * QK_WIDTH + V_WIDTH + SGU_WIDTH, 2 * QK_WIDTH + V_WIDTH + 2 * SGU_WIDTH)

kernel_name = 'hybrid_diffattn_sgu_streaming_step'


def rmsnorm(x, g):
    xf = x.astype(jnp.float32)
    y = xf * lax.rsqrt(jnp.mean(xf * xf, axis=-1, keepdims=True) + EPS)
    return (y * g.astype(jnp.float32)).astype(x.dtype)


def rope(x, pos):
    half = HEAD_DIM // 2
    inv = ROPE_THETA ** (-jnp.arange(half, dtype=jnp.float32) / half)
    ang = pos.astype(jnp.float32)[:, None] * inv[None, :]
    cos = jnp.cos(ang)[:, None, None, :]
    sin = jnp.sin(ang)[:, None, None, :]
    xf = x.astype(jnp.float32)
    x1, x2 = xf[..., :half], xf[..., half:]
    out = jnp.concatenate([x1 * cos - x2 * sin, x2 * cos + x1 * sin], axis=-1)
    return out.astype(x.dtype)


def diff_attn(q, q_pos, segs, lam):
    scale = HEAD_DIM ** -0.5
    scores = []
    for k, _, k_pos in segs:
        s = jnp.einsum('bqhcd,bkhcd->bhcqk', q, k).astype(jnp.float32) * scale
        vis = (k_pos[None, :] // CHUNK) <= (q_pos[:, None] // CHUNK)
        scores.append(jnp.where(vis, s, NEG_INF))
    p = jax.nn.softmax(jnp.concatenate(scores, axis=-1), axis=-1)
    a = p[:, :, 0] - lam * p[:, :, 1]
    o = None
    off = 0
    for k, v, _ in segs:
        n = k.shape[1]
        part = jnp.einsum('bhqk,bkhe->bqhe', a[..., off:off + n].astype(v.dtype), v)
        o = part if o is None else o + part
        off += n
    return o


def _layer(x, pos, k_past, v_past, past_pos, lambda_init, norm_mix, w_in, b_gate, q_norm, k_norm,
           lambda_q1, lambda_k1, lambda_q2, lambda_k2, subln, sgu_norm, w_s, b_s, w_o,
           norm_ffn, w_up, w_down):
    B, T, _ = x.shape
    h = rmsnorm(x, norm_mix)
    z = jnp.einsum('btd,de->bte', h, w_in)
    q, k, v, u, vs, zg = jnp.split(z, SPLITS, axis=-1)

    q = rope(rmsnorm(q.reshape(B, T, N_HEADS, 2, HEAD_DIM), q_norm), pos)
    k = rope(rmsnorm(k.reshape(B, T, N_HEADS, 2, HEAD_DIM), k_norm), pos)
    v = v.reshape(B, T, N_HEADS, 2 * HEAD_DIM)
    lam = (jnp.exp(jnp.sum(lambda_q1.astype(jnp.float32) * lambda_k1.astype(jnp.float32)))
           - jnp.exp(jnp.sum(lambda_q2.astype(jnp.float32) * lambda_k2.astype(jnp.float32)))
           + lambda_init)
    if k_past is None:
        nb = T // Q_BLOCK
        qb = jnp.swapaxes(q.reshape(B, nb, Q_BLOCK, N_HEADS, 2, HEAD_DIM), 0, 1)
        pb = pos.reshape(nb, Q_BLOCK)

        def blk(args):
            qq, pp = args
            return diff_attn(qq, pp, ((k, v, pos),), lam)

        o = lax.map(blk, (qb, pb))
        o = jnp.swapaxes(o, 0, 1).reshape(B, T, N_HEADS, 2 * HEAD_DIM)
    else:
        P = k_past.shape[1]
        kp = k_past.reshape(B, P, N_HEADS, 2, HEAD_DIM)
        o = diff_attn(q, pos, ((kp, v_past, past_pos), (k, v, pos)), lam)
    o_attn = (rmsnorm(o, subln) * (1.0 - lambda_init)).reshape(B, T, V_WIDTH)

    vs = rmsnorm(vs, sgu_norm)
    L = min(T, SGU_CHUNK)
    vc = vs.reshape(B, T // L, L, SGU_GROUPS, SGU_GROUP_DIM)
    w = jnp.tril(w_s[:, :L, :L])
    s = jnp.einsum('gts,bnsgc->bntgc', w, vc) + jnp.swapaxes(b_s[:, :L], 0, 1)[:, :, None]
    o_sgu = u * s.reshape(B, T, SGU_WIDTH)

    g = jax.nn.sigmoid(zg + b_gate)
    merged = g[..., :D_MODEL] * o_attn + g[..., D_MODEL:] * o_sgu
    x1 = x + jnp.einsum('btd,de->bte', merged, w_o)

    hf = rmsnorm(x1, norm_ffn)
    y = x1 + jnp.einsum('btf,fd->btd', jnp.square(jax.nn.relu(jnp.einsum('btd,df->btf', hf, w_up))), w_down)
    return y, k.reshape(B, T, N_HEADS, 2 * HEAD_DIM), v, vs


def setup_inputs(seed: int = 0) -> dict:
    key = jax.random.key(seed)
    ks = jax.random.split(key, 24)
    f = jnp.float32
    nrm = lambda k, shape, sc: jax.random.normal(k, shape, f) * sc
    return {
        'x_prompt': nrm(ks[0], (BATCH, SEQ, D_MODEL), 1.0),
        'x_sample': nrm(ks[1], (DEC_BATCH, DEC_SEQ, D_MODEL), 1.0),
        'cache_k_attn': nrm(ks[2], (DEPTH, DEC_BATCH, PAST_LEN, N_HEADS, 2 * HEAD_DIM), 1.0),
        'cache_v_attn': nrm(ks[3], (DEPTH, DEC_BATCH, PAST_LEN, N_HEADS, 2 * HEAD_DIM), 1.0),
        'norm_mix': 1.0 + nrm(ks[4], (DEPTH, D_MODEL), 0.02),
        'w_in': nrm(ks[5], (DEPTH, D_MODEL, IN_WIDTH), D_MODEL ** -0.5),
        'b_gate': nrm(ks[6], (DEPTH, 2 * D_MODEL), 0.02),
        'q_norm': 1.0 + nrm(ks[7], (DEPTH, HEAD_DIM), 0.02),
        'k_norm': 1.0 + nrm(ks[8], (DEPTH, HEAD_DIM), 0.02),
        'lambda_q1': nrm(ks[9], (DEPTH, HEAD_DIM), 0.1),
        'lambda_k1': nrm(ks[10], (DEPTH, HEAD_DIM), 0.1),
        'lambda_q2': nrm(ks[11], (DEPTH, HEAD_DIM), 0.1),
        'lambda_k2': nrm(ks[12], (DEPTH, HEAD_DIM), 0.1),
        'subln': 1.0 + nrm(ks[13], (DEPTH, 2 * HEAD_DIM), 0.02),
        'sgu_norm': 1.0 + nrm(ks[14], (DEPTH, SGU_WIDTH), 0.02),
        'w_s': nrm(ks[15], (DEPTH, SGU_GROUPS, SGU_CHUNK, SGU_CHUNK), SGU_CHUNK ** -0.5),
        'b_s': 1.0 + nrm(ks[16], (DEPTH, SGU_GROUPS, SGU_CHUNK), 0.02),
        'w_o': nrm(ks[17], (DEPTH, D_MODEL, D_MODEL), D_MODEL ** -0.5),
        'norm_ffn': 1.0 + nrm(ks[18], (DEPTH, D_MODEL), 0.02),
        'w_up': nrm(ks[19], (DEPTH, D_MODEL, D_FF), D_MODEL ** -0.5),
        'w_down': nrm(ks[20], (DEPTH, D_FF, D_MODEL), D_FF ** -0.5),
    }


def reference(x_prompt, x_sample, cache_k_attn, cache_v_attn, norm_mix, w_in, b_gate, q_norm, k_norm,
              lambda_q1, lambda_k1, lambda_q2, lambda_k2, subln, sgu_norm, w_s, b_s, w_o,
              norm_ffn, w_up, w_down):
    pos_p = jnp.arange(SEQ, dtype=jnp.int32)
    pos_s = PAST_LEN + jnp.arange(DEC_SEQ, dtype=jnp.int32)
    past_pos = jnp.arange(PAST_LEN, dtype=jnp.int32)
    xp, xs = x_prompt, x_sample
    kp_l, vp_l, ks_l, vs_l, sg_l = [], [], [], [], []
    for l in range(DEPTH):
        lambda_init = 0.8 - 0.6 * math.exp(-0.3 * l)
        wl = (norm_mix[l], w_in[l], b_gate[l], q_norm[l], k_norm[l], lambda_q1[l], lambda_k1[l],
              lambda_q2[l], lambda_k2[l], subln[l], sgu_norm[l], w_s[l], b_s[l], w_o[l],
              norm_ffn[l], w_up[l], w_down[l])
        xp, kp, vp, _ = _layer(xp, pos_p, None, None, None, lambda_init, *wl)
        xs, ksm, vsm, sgv = _layer(xs, pos_s, cache_k_attn[l], cache_v_attn[l], past_pos, lambda_init, *wl)
        kp_l.append(kp); vp_l.append(vp); ks_l.append(ksm); vs_l.append(vsm); sg_l.append(sgv)
    new_k_prompt = jnp.stack(kp_l)
    new_v_prompt = jnp.stack(vp_l)
    new_k_sample = jnp.stack(ks_l)
    new_v_sample = jnp.stack(vs_l)
    new_sgu_v_sample = jnp.stack(sg_l)
    return (xp, xs, new_k_prompt, new_v_prompt, new_k_sample, new_v_sample, new_sgu_v_sample)
```

```python
import math
from contextlib import ExitStack

import numpy as np
import concourse.bass as bass
import concourse.mybir as mybir
from concourse.bass_utils import run_bass_kernel_spmd

F32 = mybir.dt.float32
BF16 = mybir.dt.bfloat16
AF = mybir.ActivationFunctionType
ALU = mybir.AluOpType
AX = mybir.AxisListType

EPS = 1e-6
ROPE_THETA = 10000.0
N_CORES = 8


class Cfg:
    def __init__(self, D=4096, H=16, T=4096, P=4096, F=16384, G=8, SB=16, ST=16, NB=2):
        self.D, self.H, self.T, self.P, self.F, self.G, self.SB, self.ST, self.NB = D, H, T, P, F, G, SB, ST, NB
        self.KC = D // 128
        self.NT = T // 128
        self.NJ = self.NT // 4
        self.NOWN = self.NJ + 1
        self.SROWS = 64
        self.OWNROWS = self.NJ * 128 + self.SROWS
        self.PT = P // 128
        self.CB = 256
        self.NCB = D // self.CB
        self.GD = D // G
        self.lambda_init = 0.8 - 0.6 * math.exp(-0.3 * 0)

    def rows(self, ti):
        return 128 if ti < self.NJ else self.SROWS


class Cell:
    __slots__ = ("w", "r")

    def __init__(self):
        self.w = None
        self.r = {}


class Buf:
    def __init__(self, ncells=1, cells=None):
        self.cells = cells if cells is not None else [Cell() for _ in range(ncells)]

    def sub(self, i, j=None):
        return Buf(cells=self.cells[i:(i + 1 if j is None else j)])


ENGS = ("pe", "act", "dve", "pool", "sp")


class Prog:
    def __init__(self, nc, stack):
        self.nc = nc
        self.stack = stack
        self.q = {e: [] for e in ENGS}
        self.sems = {}
        self.count = {}
        self.waited = {e: {} for e in ENGS}
        for e in ENGS:
            self._sem(e)
        import os as _os
        for i in range(int(_os.environ.get("MK_DUMMY", "0"))):
            self._sem(f"dummy{i}")

    def _sem(self, key):
        if key not in self.sems:
            self.sems[key] = self.stack.enter_context(self.nc.semaphore("s_" + str(key)))
            self.count[key] = 0
        return key

    def dsem(self, name):
        return self._sem("d_" + name)

    def op(self, eng, fn, reads=(), writes=(), sem=None, ninc=1):
        needs = {}

        def add(ev):
            if ev is not None and needs.get(ev[0], 0) < ev[1]:
                needs[ev[0]] = ev[1]

        wcells = set()
        for b in writes:
            for c in b.cells:
                wcells.add(id(c))
                add(c.w)
                for k, v in c.r.items():
                    add((k, v))
        for b in reads:
            for c in b.cells:
                add(c.w)
        wt = self.waited[eng]
        waits = [(k, v) for k, v in needs.items() if wt.get(k, 0) < v]
        for k, v in waits:
            wt[k] = v
        if sem is None:
            key, amt = eng, 1
        else:
            key, amt = sem, 16 * ninc
        self.count[key] += amt
        val = self.count[key]
        for b in writes:
            for c in b.cells:
                c.w = (key, val)
                c.r = {}
        for b in reads:
            for c in b.cells:
                if id(c) not in wcells:
                    c.r[key] = val
        self.q[eng].append((waits, fn, key, sem is None))

    def flush(self):
        nc = self.nc
        attr = {"pe": "tensor", "act": "scalar", "dve": "vector", "pool": "gpsimd", "sp": "sync"}
        with nc.Block() as block:
            for name in ENGS:
                ops = self.q[name]

                def body(e, ops=ops):
                    for waits, fn, key, own in ops:
                        for k, v in waits:
                            e.wait_ge(self.sems[k], v)
                        if fn is None:
                            continue
                        r = fn(e)
                        if own:
                            if isinstance(r, (list, tuple)):
                                r = r[-1]
                            r.then_inc(self.sems[key], 1)
                        else:
                            for ins in r:
                                ins.then_inc(self.sems[key], 16)

                getattr(block, attr[name])(body)
        self.q = {e: [] for e in ENGS}
        for e in ENGS:
            waits = [(k, v) for k, v in self.count.items() if v > 0 and self.waited[e].get(k, 0) < v]
            for k, v in waits:
                self.waited[e][k] = v
            self.q[e].append((waits, None, None, False))

    def dma(self, q, out, in_, reads, writes, sem):
        self.op(q, lambda e: [e.dma_start(out=out, in_=in_)], reads, writes, sem=sem)

    def mm_group(self, mms, reads, writes):
        def fn(e, mms=mms):
            last = None
            for (o, l, r, st, sp) in mms:
                last = e.matmul(o, lhsT=l, rhs=r, start=st, stop=sp)
            return last
        self.op("pe", fn, reads, writes)

    def transposes(self, items, ident, reads, writes):
        def fn(e, items=items):
            last = None
            for (o, i, n) in items:
                last = e.transpose(o, i, ident[:n, :n])
            return last
        self.op("pe", fn, reads, writes)


def build_program(cfg, stop=9):
    D, H, T, P, F, G = cfg.D, cfg.H, cfg.T, cfg.P, cfg.F, cfg.G
    KC, NT, NJ, NOWN, SROWS, OWNROWS, PT = cfg.KC, cfg.NT, cfg.NJ, cfg.NOWN, cfg.SROWS, cfg.OWNROWS, cfg.PT
    CB, NCB, GD = cfg.CB, cfg.NCB, cfg.GD
    HW = 256
    scale = 128 ** -0.5

    nc = bass.Bass("TRN2", target_bir_lowering=False)

    def din(name, shape, dt=F32):
        return nc.dram_tensor(name, list(shape), dt, kind="ExternalInput").ap()

    def dout(name, shape, dt=F32):
        return nc.dram_tensor(name, list(shape), dt, kind="ExternalOutput").ap()

    x_all = din("x_all", [T, D])
    x_own = din("x_own", [OWNROWS, D])
    cache_k = din("cache_k", [2, P, H * HW])
    cache_v = din("cache_v", [2, P, H * HW])
    cs_all = din("cs_all", [T, 256])
    cs_own = din("cs_own", [OWNROWS, 256])
    masks_d = din("masks", [128, 4 * 128])
    ident_d = din("ident", [128, 128])
    tril_d = din("tril", [128, 128])
    trils_d = din("trils", [64, 64])
    w_in = din("w_in", [D, 7 * D])
    bgate_bc = din("bgate_bc", [128, 2 * D])
    qn_bc = din("qn_bc", [128, 128])
    kn_bc = din("kn_bc", [128, 128])
    lam_d = din("lam4", [128, 4 * 128])
    subln_bc = din("subln_bc", [128, 256])
    sgu_bc = din("sgu_bc", [128, D])
    wsT_d = din("wsT", [128, G * 128])
    wsS_d = din("wsS", [64, G * 64])
    bs_t = din("bs_t", [128, G])
    bs_s = din("bs_s", [64, G])
    w_o = din("w_o", [D, D])
    nmix_pk = din("nmix_pk", [128, KC])
    nffn_pk = din("nffn_pk", [128, KC])
    w_up = din("w_up", [D, F])
    w_down = din("w_down", [F, D])

    y_own = dout("y_own", [OWNROWS, D])
    k_all = dout("k_all", [T, H * HW])
    v_all = dout("v_all", [T, H * HW])
    k_s = dout("k_s", [SROWS, H * HW])
    v_s = dout("v_s", [SROWS, H * HW])
    sgu_s = dout("sgu_s", [SROWS, D])

    hT_all = nc.dram_tensor("hT_all", [NT, 128, KC * 128], BF16).ap()
    hT_own = nc.dram_tensor("hT_own", [NOWN, 128, KC * 128], BF16).ap()
    oatt_s = nc.dram_tensor("oatt_s", [OWNROWS, D], BF16).ap()
    vs32_s = nc.dram_tensor("vs32_s", [SROWS, D], F32).ap()

    with ExitStack() as top:
        Pg = Prog(nc, top)

        def sb(stack, name, shape, dt):
            return stack.enter_context(nc.sbuf_tensor(name, list(shape), dt))

        def ps(stack, name, shape, dt):
            return stack.enter_context(nc.psum_tensor(name, list(shape), dt))

        FB_ = [ps(top, f"F{i}", [128, 512], F32) for i in range(6)]
        FBb = [Buf() for _ in range(6)]
        TB_ = [ps(top, f"T{i}", [128, 1024], BF16) for i in range(2)]
        TBb = [Buf() for _ in range(2)]

        ident = sb(top, "ident_sb", [128, 128], BF16)
        identb = Buf()
        Pg.dma("pool", ident[:], ident_d, [], [identb], Pg.dsem("ident"))
        epsc = sb(top, "epsc", [128, 1], F32)
        epsb = Buf()
        Pg.op("pool", lambda e: e.memset(epsc[:], EPS), [], [epsb])

        def rstd_ops(ss_ap, out_ap, ssb, outb, n, eng="act"):
            Pg.op("act", lambda e: e.activation(out=out_ap, in_=ss_ap, func=AF.Ln, scale=1.0 / n, bias=epsc[:ss_ap.shape[0], :]), [ssb, epsb], [outb])
            Pg.op("act", lambda e: e.activation(out=out_ap, in_=out_ap, func=AF.Exp, scale=-0.5), [outb], [outb])

        with ExitStack() as ph:
            xs = [sb(ph, f"p0_xs{i}", [128, D], F32) for i in range(2)]
            xsb = [Buf() for _ in range(2)]
            xsem = [Pg.dsem(f"p0xs{i}") for i in range(2)]
            junk = sb(ph, "p0_junk", [128, D], BF16)
            junkb = Buf()
            hbf = [sb(ph, f"p0_hbf{i}", [128, D], BF16) for i in range(2)]
            hbfb = [Buf() for _ in range(2)]
            hst = [sb(ph, f"p0_hst{i}", [128, KC, 128], BF16) for i in range(2)]
            hstb = [Buf() for _ in range(2)]
            hsem = [Pg.dsem(f"p0hst{i}") for i in range(2)]
            ss = [sb(ph, f"p0_ss{i}", [128, 1], F32) for i in range(2)]
            ssb = [Buf() for _ in range(2)]
            rs = [sb(ph, f"p0_rs{i}", [128, 1], F32) for i in range(2)]
            rsb = [Buf() for _ in range(2)]
            gpk = sb(ph, "p0_gpk", [128, KC], F32)
            gpkb = Buf()
            Pg.dma("sp", gpk[:], nmix_pk, [], [gpkb], Pg.dsem("p0gpk"))

            jobs = [(x_all, i, 128, hT_all, i) for i in range(NT)] + \
                   [(x_own, i, cfg.rows(i), hT_own, i) for i in range(NOWN)]
            tcount = 0
            for n, (src, ti, rows, dst, di) in enumerate(jobs):
                s = n % 2
                Pg.dma("sp", xs[s][:rows, :], src[ti * 128: ti * 128 + rows, :], [], [xsb[s]], xsem[s])
                Pg.op("pool", lambda e, s=s: e.memset(ss[s][:], 0.0), [], [ssb[s]])
                Pg.op("act", lambda e, s=s, rows=rows: e.activation(out=junk[:rows, :], in_=xs[s][:rows, :], func=AF.Square,
                                                                    accum_out=ss[s][:rows, :]),
                      [xsb[s]], [junkb, ssb[s]])
                rstd_ops(ss[s][:rows, :], rs[s][:rows, :], ssb[s], rsb[s], D)
                Pg.op("dve", lambda e, s=s, rows=rows: e.tensor_scalar(out=hbf[s][:rows, :], in0=xs[s][:rows, :],
                                                                       scalar1=rs[s][:rows, :], scalar2=None, op0=ALU.mult),
                      [xsb[s], rsb[s]], [hbfb[s]])
                if rows < 128:
                    Pg.op("pool", lambda e, s=s: e.memset(hst[s][:], 0.0), [], [hstb[s]])
                for g0 in range(0, KC, 8):
                    g1 = min(KC, g0 + 8)
                    tb = tcount % 2
                    tcount += 1
                    items = [(TB_[tb][:, (kc - g0) * 128:(kc - g0) * 128 + rows], hbf[s][:rows, kc * 128:(kc + 1) * 128], rows)
                             for kc in range(g0, g1)]
                    Pg.transposes(items, ident, [hbfb[s], identb], [TBb[tb]])
                    Pg.op("dve", lambda e, s=s, tb=tb, g0=g0, g1=g1, rows=rows: e.tensor_tensor(
                        out=hst[s][:, g0:g1, :rows],
                        in0=TB_[tb][:, 0:(g1 - g0) * 128].rearrange("p (k t) -> p k t", t=128)[:, :, :rows],
                        in1=gpk[:, g0:g1].unsqueeze(2).to_broadcast([128, g1 - g0, rows]), op=ALU.mult),
                        [TBb[tb], gpkb], [hstb[s]])
                Pg.dma("sp", dst[di], hst[s][:].rearrange("p k t -> p (k t)"), [hstb[s]], [], hsem[s])
            Pg.flush()
            if stop == 0:
                Pg.flush()
                return nc

        with ExitStack() as ph:
            wq = sb(ph, "p1_wq", [128, KC, 256], BF16)
            wqb = Buf()
            wqs = Pg.dsem("p1wq")
            wkv = sb(ph, "p1_wkv", [128, KC, 512], BF16)
            wkvb = Buf()
            wkvs = Pg.dsem("p1wkv")
            NHT = 3
            hts = [sb(ph, f"p1_ht{i}", [128, KC, 128], BF16) for i in range(NHT)]
            htb = [Buf() for _ in range(NHT)]
            htsem = [Pg.dsem(f"p1ht{i}") for i in range(NHT)]
            cst = [sb(ph, f"p1_cs{i}", [128, 256], F32) for i in range(NHT)]
            cstb = [Buf() for _ in range(NHT)]
            cssem = [Pg.dsem(f"p1cs{i}") for i in range(NHT)]
            KT = sb(ph, "p1_KT", [128, 2, T], BF16)
            KTb = Buf(NT)
            Vh = sb(ph, "p1_V", [128, NT, 257], BF16)
            Vb = Buf(NT)
            QT = sb(ph, "p1_QT", [128, 2, NJ * 128], BF16)
            QTb = Buf(NJ)
            QTs = sb(ph, "p1_QTs", [128, 2, SROWS], BF16)
            QTsb = Buf()
            KTs = sb(ph, "p1_KTs", [128, 2, SROWS], BF16)
            KTsb = Buf()
            Vs = sb(ph, "p1_Vs", [SROWS, 257], BF16)
            Vsb = Buf()
            Kc = sb(ph, "p1_Kc", [128, PT, 256], BF16)
            Kcb = Buf()
            Kcs = Pg.dsem("p1kc")
            KTc = sb(ph, "p1_KTc", [128, 2, P], BF16)
            KTcb = Buf()
            Vc = sb(ph, "p1_Vc", [128, PT, 257], BF16)
            Vcb = Buf()
            Vcs = Pg.dsem("p1vc")
            NST = 3
            kvs = [sb(ph, f"p1_kvs{i}", [128, 768], F32) for i in range(NST)]
            kvsb = [Buf() for _ in range(NST)]
            kvsem = [Pg.dsem(f"p1kvs{i}") for i in range(NST)]
            ko = [sb(ph, f"p1_ko{i}", [128, 512], F32) for i in range(NST)]
            kob = [Buf() for _ in range(NST)]
            kosem = [Pg.dsem(f"p1ko{i}") for i in range(NST)]
            tmpa = [sb(ph, f"p1_tmpa{i}", [128, 256], F32) for i in range(2)]
            tmpab = [Buf() for _ in range(2)]
            tmpb = [sb(ph, f"p1_tmpb{i}", [128, 256], F32) for i in range(2)]
            tmpbb = [Buf() for _ in range(2)]
            knt = [sb(ph, f"p1_kn{i}", [128, 256], F32) for i in range(2)]
            kntb = [Buf() for _ in range(2)]
            kbf = [sb(ph, f"p1_kbf{i}", [128, 256], BF16) for i in range(4)]
            kbfb = [Buf() for _ in range(4)]
            ss4 = [sb(ph, f"p1_ss4{i}", [128, 4], F32) for i in range(2)]
            ss4b = [Buf() for _ in range(2)]
            rs4 = [sb(ph, f"p1_rs4{i}", [128, 4], F32) for i in range(2)]
            rs4b = [Buf() for _ in range(2)]
            PTt = [sb(ph, f"p1_PT{i}", [128, 512], BF16) for i in range(3)]
            PTb = [Buf() for _ in range(3)]
            PTn = sb(ph, "p1_PTn", [SROWS, 32], BF16)
            PTnb = Buf()
            maskt = sb(ph, "p1_mask", [128, 4, 128], BF16)
            maskb = Buf()
            qg = sb(ph, "p1_qg", [128, 128], F32)
            kg = sb(ph, "p1_kg", [128, 128], F32)
            gb = Buf()
            lam4 = sb(ph, "p1_lam4", [128, 4, 128], F32)
            lamt = sb(ph, "p1_lamt", [128, 2, 128], F32)
            lam2 = sb(ph, "p1_lam2", [128, 2], F32)
            neglam = sb(ph, "p1_neglam", [128, 1], F32)
            lamb = Buf()
            slg = sb(ph, "p1_slg", [128, 256], F32)
            slgb = Buf()
            fin_o0 = sb(ph, "p1_o0", [128, 256], F32)
            fin_o = sb(ph, "p1_o", [128, 256], F32)
            fin_sq = sb(ph, "p1_fsq", [128, 256], F32)
            finb = Buf()
            fin_r = sb(ph, "p1_fr", [128, 8], F32)
            finrb = Buf()
            oat = [sb(ph, f"p1_oat{i}", [128, 256], BF16) for i in range(2)]
            oatb = [Buf() for _ in range(2)]
            oatsem = [Pg.dsem(f"p1oat{i}") for i in range(2)]
            oats = sb(ph, "p1_oats", [SROWS, 256], BF16)
            oatsb = Buf()
            oatssem = Pg.dsem("p1oats")

            Pg.dma("pool", maskt[:].rearrange("p a b -> p (a b)"), masks_d, [], [maskb], Pg.dsem("p1mask"))
            csem = Pg.dsem("p1const")
            Pg.op("sp", lambda e: [e.dma_start(out=qg[:], in_=qn_bc), e.dma_start(out=kg[:], in_=kn_bc),
                                   e.dma_start(out=lam4[:].rearrange("p a b -> p (a b)"), in_=lam_d),
                                   e.dma_start(out=slg[:], in_=subln_bc)], [], [gb, lamb, slgb], sem=csem, ninc=4)
            Pg.op("dve", lambda e: e.tensor_tensor(out=lamt[:], in0=lam4[:].rearrange("p (a b) d -> p a b d", b=2)[:, :, 0, :], in1=lam4[:].rearrange("p (a b) d -> p a b d", b=2)[:, :, 1, :], op=ALU.mult),
                  [lamb], [lamb])
            Pg.op("dve", lambda e: e.tensor_reduce(out=lam2[:], in_=lamt[:], axis=AX.X, op=ALU.add), [lamb], [lamb])
            Pg.op("act", lambda e: e.activation(out=lam2[:], in_=lam2[:], func=AF.Exp), [lamb], [lamb])
            Pg.op("dve", lambda e: e.tensor_tensor(out=neglam[:], in0=lam2[:, 1:2], in1=lam2[:, 0:1], op=ALU.subtract),
                  [lamb], [lamb])
            Pg.op("dve", lambda e: e.tensor_scalar(out=neglam[:], in0=neglam[:], scalar1=-cfg.lambda_init, scalar2=None,
                                                   op0=ALU.add), [lamb], [lamb])
            Pg.op("dve", lambda e: e.tensor_scalar(out=slg[:], in0=slg[:], scalar1=1.0 - cfg.lambda_init, scalar2=None,
                                                   op0=ALU.mult), [slgb], [slgb])
            Pg.op("pool", lambda e: e.memset(Vh[:, :, 256:257], 1.0), [], [Vb])
            Pg.op("pool", lambda e: e.memset(Vc[:, :, 256:257], 1.0), [], [Vcb])
            Pg.op("pool", lambda e: e.memset(Vs[:, 256:257], 1.0), [], [Vsb])
            Pg.op("pool", lambda e: e.memset(oats[:], 0.0), [], [oatsb])

            w3 = w_in.rearrange("(k p) c -> p k c", p=128)

            state = {"st": 0, "tb": 0, "fb": 0, "ht": 0, "t2": 0, "kb": 0}
            pend = []
            DL = 2

            def load_ht(src, idx, cs_src, rows):
                i = state["ht"] % NHT
                state["ht"] += 1
                Pg.dma("sp", hts[i][:].rearrange("p k t -> p (k t)"), src[idx], [], [htb[i]], htsem[i])
                Pg.dma("sp", cst[i][:rows, :], cs_src, [], [cstb[i]], cssem[i])
                return i

            def norm_rope(src_ap, srcb, nsub, rows, gain, csi, dst_f32, dstb, dst_bf, dst_bfb):
                t = state["t2"] % 2
                state["t2"] += 1
                W = nsub * 128
                v3 = lambda ap: ap.rearrange("p (s d) -> p s d", d=128)
                Pg.op("dve", lambda e: e.tensor_tensor(out=tmpa[t][:rows, :W], in0=src_ap, in1=src_ap, op=ALU.mult),
                      [srcb], [tmpab[t]])
                Pg.op("dve", lambda e: e.tensor_reduce(out=ss4[t][:rows, :nsub], in_=v3(tmpa[t][:rows, :W]), axis=AX.X, op=ALU.add),
                      [tmpab[t]], [ss4b[t]])
                rstd_ops(ss4[t][:rows, :nsub], rs4[t][:rows, :nsub], ss4b[t], rs4b[t], 128)
                for s_ in range(nsub):
                    Pg.op("dve", lambda e, s_=s_: e.scalar_tensor_tensor(
                        out=knt[t][:rows, s_ * 128:(s_ + 1) * 128], in0=src_ap[:, s_ * 128:(s_ + 1) * 128],
                        scalar=rs4[t][:rows, s_:s_ + 1], in1=gain[:rows, :], op0=ALU.mult, op1=ALU.mult),
                        [srcb, rs4b[t], gb], [kntb[t].sub(0)] if False else [kntb[t]])
                kn3 = v3(knt[t][:rows, :W])
                cos_b = cst[csi][:rows, 0:128].unsqueeze(1).to_broadcast([rows, nsub, 128])
                Pg.op("dve", lambda e: e.tensor_tensor(out=v3(tmpa[t][:rows, :W]), in0=kn3, in1=cos_b, op=ALU.mult),
                      [kntb[t], cstb[csi]], [tmpab[t]])
                sinA = cst[csi][:rows, 128:192].unsqueeze(1).to_broadcast([rows, nsub, 64])
                sinB = cst[csi][:rows, 192:256].unsqueeze(1).to_broadcast([rows, nsub, 64])
                tb3 = v3(tmpb[t][:rows, :W])
                Pg.op("pool", lambda e: e.tensor_tensor(out=tb3[:, :, 0:64], in0=kn3[:, :, 64:128], in1=sinA, op=ALU.mult),
                      [kntb[t], cstb[csi]], [tmpbb[t]])
                Pg.op("pool", lambda e: e.tensor_tensor(out=tb3[:, :, 64:128], in0=kn3[:, :, 0:64], in1=sinB, op=ALU.mult),
                      [kntb[t], cstb[csi], tmpbb[t]], [tmpbb[t]])
                Pg.op("dve", lambda e: e.tensor_tensor(out=dst_f32, in0=tmpa[t][:rows, :W], in1=tmpb[t][:rows, :W], op=ALU.add),
                      [tmpab[t], tmpbb[t]], [dstb])
                Pg.op("pool", lambda e: e.tensor_copy(out=dst_bf, in_=dst_f32), [dstb], [dst_bfb])

            def finalize(acc0, acc1, accb0, accb1, p0, p1, dst_bf, dstb):
                sl = slice(p0, p1)
                Pg.op("dve", lambda e: e.reciprocal(out=fin_r[sl, 0:1], in_=acc0[sl, 256:257]), [accb0], [finrb])
                Pg.op("dve", lambda e: e.reciprocal(out=fin_r[sl, 1:2], in_=acc1[sl, 256:257]), [accb1, finrb], [finrb])
                Pg.op("dve", lambda e: e.tensor_scalar(out=fin_o0[sl, :], in0=acc0[sl, 0:256], scalar1=fin_r[sl, 0:1], scalar2=None,
                                                       op0=ALU.mult), [accb0, finrb], [finb])
                Pg.op("dve", lambda e: e.tensor_tensor(out=fin_r[sl, 2:3], in0=fin_r[sl, 1:2], in1=neglam[sl, :], op=ALU.mult),
                      [finrb, lamb], [finrb])
                Pg.op("dve", lambda e: e.scalar_tensor_tensor(out=fin_o[sl, :], in0=acc1[sl, 0:256], scalar=fin_r[sl, 2:3],
                                                              in1=fin_o0[sl, :], op0=ALU.mult, op1=ALU.add),
                      [accb1, finrb, finb], [finb])
                Pg.op("dve", lambda e: e.tensor_tensor(out=fin_sq[sl, :], in0=fin_o[sl, :], in1=fin_o[sl, :], op=ALU.mult),
                      [finb], [finb])
                Pg.op("dve", lambda e: e.tensor_reduce(out=fin_r[sl, 3:4], in_=fin_sq[sl, :], axis=AX.X, op=ALU.add),
                      [finb, finrb], [finrb])
                rstd_ops(fin_r[sl, 3:4], fin_r[sl, 4:5], finrb, finrb, 256)
                Pg.op("dve", lambda e: e.scalar_tensor_tensor(out=dst_bf, in0=fin_o[sl, :], scalar=fin_r[sl, 4:5],
                                                              in1=slg[sl, :], op0=ALU.mult, op1=ALU.mult),
                      [finb, finrb, slgb], [dstb])

            for h in range(H):
                Pg.op("pool", lambda e, h=h: [e.dma_start(out=wkv[:, :, 0:256], in_=w3[:, :, D + h * HW: D + (h + 1) * HW]),
                                              e.dma_start(out=wkv[:, :, 256:512], in_=w3[:, :, 2 * D + h * HW: 2 * D + (h + 1) * HW])],
                      [], [wkvb], sem=wkvs, ninc=2)
                Pg.op("pool", lambda e, h=h: [e.dma_start(out=wq[:], in_=w3[:, :, h * HW:(h + 1) * HW])], [], [wqb], sem=wqs)

                nxt = load_ht(hT_all, 0, cs_all[0:128, :], 128)
                for i in range(NT):
                    hi = nxt
                    if i + 1 < NT:
                        nxt = load_ht(hT_all, i + 1, cs_all[(i + 1) * 128:(i + 2) * 128, :], 128)
                    fb = state["fb"] % 4
                    state["fb"] += 1
                    st = state["st"] % NST
                    state["st"] += 1
                    Pg.mm_group([(FB_[fb][:, 0:512], hts[hi][:, kc, :], wkv[:, kc, :], kc == 0, kc == KC - 1) for kc in range(KC)],
                                [htb[hi], wkvb], [FBb[fb]])
                    while len(pend) >= DL:
                        pend.pop(0)()
                    Pg.op("act", lambda e, fb=fb, st=st: e.activation(out=kvs[st][:, 0:512], in_=FB_[fb][:, 0:512], func=AF.Copy),
                          [FBb[fb]], [kvsb[st]])
                    t2 = state["kb"] % 4
                    state["kb"] += 1
                    norm_rope(kvs[st][:, 0:256], kvsb[st], 2, 128, kg, hi, ko[st][:, 0:256], kob[st], kbf[t2][:, 0:256], kbfb[t2])

                    def tr_k(t2=t2, i=i):
                        tb = state["tb"] % 2
                        state["tb"] += 1
                        Pg.transposes([(TB_[tb][:, s_ * 128:(s_ + 1) * 128], kbf[t2][:, s_ * 128:(s_ + 1) * 128], 128) for s_ in range(2)],
                                      ident, [kbfb[t2], identb], [TBb[tb]])
                        Pg.op("act", lambda e, tb=tb, i=i: e.activation(out=KT[:, :, i * 128:(i + 1) * 128],
                                                                        in_=TB_[tb][:, 0:256].rearrange("p (s t) -> p s t", t=128), func=AF.Copy),
                              [TBb[tb]], [KTb.sub(i)])
                    pend.append(tr_k)
                    Pg.op("pool", lambda e, st=st, i=i: e.tensor_copy(out=Vh[:, i, 0:256], in_=kvs[st][:, 256:512]),
                          [kvsb[st]], [Vb.sub(i)])
                    Pg.dma("pool", k_all[i * 128:(i + 1) * 128, h * HW:(h + 1) * HW], ko[st][:, 0:256], [kob[st]], [], kosem[st])
                    Pg.dma("pool", v_all[i * 128:(i + 1) * 128, h * HW:(h + 1) * HW], kvs[st][:, 256:512], [kvsb[st]], [], kvsem[st])

                nxt = load_ht(hT_own, 0, cs_own[0:cfg.rows(0), :], cfg.rows(0))
                for j in range(NOWN):
                    rows = cfg.rows(j)
                    hi = nxt
                    if j + 1 < NOWN:
                        r2 = cfg.rows(j + 1)
                        nxt = load_ht(hT_own, j + 1, cs_own[(j + 1) * 128:(j + 1) * 128 + r2, :], r2)
                    fb = state["fb"] % 4
                    state["fb"] += 1
                    st = state["st"] % NST
                    state["st"] += 1
                    Pg.mm_group([(FB_[fb][:rows, 0:256], hts[hi][:, kc, :rows], wq[:, kc, :], kc == 0, kc == KC - 1) for kc in range(KC)],
                                [htb[hi], wqb], [FBb[fb]])
                    while len(pend) >= DL:
                        pend.pop(0)()
                    Pg.op("act", lambda e, fb=fb, st=st, rows=rows: e.activation(out=kvs[st][:rows, 512:768], in_=FB_[fb][:rows, 0:256], func=AF.Copy),
                          [FBb[fb]], [kvsb[st]])
                    if j < NJ:
                        t2 = state["kb"] % 4
                        state["kb"] += 1
                        norm_rope(kvs[st][:, 512:768], kvsb[st], 2, 128, qg, hi, ko[st][:, 256:512], kob[st], kbf[t2][:, 0:256], kbfb[t2])

                        def tr_q(t2=t2, j=j):
                            tb = state["tb"] % 2
                            state["tb"] += 1
                            Pg.transposes([(TB_[tb][:, s_ * 128:(s_ + 1) * 128], kbf[t2][:, s_ * 128:(s_ + 1) * 128], 128) for s_ in range(2)],
                                          ident, [kbfb[t2], identb], [TBb[tb]])
                            Pg.op("act", lambda e, tb=tb, j=j: e.activation(out=QT[:, :, j * 128:(j + 1) * 128],
                                                                            in_=TB_[tb][:, 0:256].rearrange("p (s t) -> p s t", t=128), func=AF.Copy),
                                  [TBb[tb]], [QTb.sub(j)])
                        pend.append(tr_q)
                    else:
                        while pend:
                            pend.pop(0)()
                        R = SROWS
                        fb2 = state["fb"] % 4
                        state["fb"] += 1
                        Pg.mm_group([(FB_[fb2][:R, 0:512], hts[hi][:, kc, :R], wkv[:, kc, :], kc == 0, kc == KC - 1) for kc in range(KC)],
                                    [htb[hi], wkvb], [FBb[fb2]])
                        Pg.op("act", lambda e, fb2=fb2, st=st: e.activation(out=kvs[st][:R, 0:512], in_=FB_[fb2][:R, 0:512], func=AF.Copy),
                              [FBb[fb2]], [kvsb[st]])
                        t2 = state["kb"] % 4
                        state["kb"] += 1
                        norm_rope(kvs[st][:R, 512:768], kvsb[st], 2, R, qg, hi, ko[st][:R, 256:512], kob[st], kbf[t2][:R, 0:256], kbfb[t2])
                        tb = state["tb"] % 2
                        state["tb"] += 1
                        Pg.transposes([(TB_[tb][:, s_ * R:(s_ + 1) * R], kbf[t2][:R, s_ * 128:(s_ + 1) * 128], R) for s_ in range(2)],
                                      ident, [kbfb[t2], identb], [TBb[tb]])
                        Pg.op("act", lambda e, tb=tb: e.activation(out=QTs[:], in_=TB_[tb][:, 0:2 * R].rearrange("p (s t) -> p s t", t=R), func=AF.Copy),
                              [TBb[tb]], [QTsb])
                        t2 = state["kb"] % 4
                        state["kb"] += 1
                        norm_rope(kvs[st][:R, 0:256], kvsb[st], 2, R, kg, hi, ko[st][:R, 0:256], kob[st], kbf[t2][:R, 0:256], kbfb[t2])
                        tb = state["tb"] % 2
                        state["tb"] += 1
                        Pg.transposes([(TB_[tb][:, s_ * R:(s_ + 1) * R], kbf[t2][:R, s_ * 128:(s_ + 1) * 128], R) for s_ in range(2)],
                                      ident, [kbfb[t2], identb], [TBb[tb]])
                        Pg.op("act", lambda e, tb=tb: e.activation(out=KTs[:], in_=TB_[tb][:, 0:2 * R].rearrange("p (s t) -> p s t", t=R), func=AF.Copy),
                              [TBb[tb]], [KTsb])
                        Pg.op("pool", lambda e, st=st: e.tensor_copy(out=Vs[:, 0:256], in_=kvs[st][:R, 256:512]), [kvsb[st]], [Vsb])
                        Pg.dma("pool", k_s[:, h * HW:(h + 1) * HW], ko[st][:R, 0:256], [kob[st]], [], kosem[st])
                        Pg.dma("pool", v_s[:, h * HW:(h + 1) * HW], kvs[st][:R, 256:512], [kvsb[st]], [], kvsem[st])

                for j in range(NJ):
                    nkt = 4 * j + 4
                    npair = nkt // 2
                    a0, a1 = 2 + 2 * (j % 2), 3 + 2 * (j % 2)

                    def scores(p, j=j):
                        sbk = p % 2
                        mms = []
                        for kl in range(2):
                            kt = 2 * p + kl
                            for s_ in range(2):
                                c0 = (kl * 2 + s_) * 128
                                mms.append((FB_[sbk][:, c0:c0 + 128], KT[:, s_, kt * 128:(kt + 1) * 128], QT[:, s_, j * 128:(j + 1) * 128], True, True))
                        Pg.mm_group(mms, [KTb.sub(2 * p, 2 * p + 2), QTb.sub(j)], [FBb[sbk]])

                    scores(0)
                    if npair > 1:
                        scores(1)
                    for p in range(npair):
                        sbk = p % 2
                        pt = (j * 64 + p) % 3
                        Pg.op("act", lambda e, sbk=sbk, pt=pt: e.activation(out=PTt[pt][:], in_=FB_[sbk][:, 0:512], func=AF.Exp, scale=scale),
                              [FBb[sbk]], [PTb[pt]])
                        if p >= npair - 2:
                            m0 = 2 * (p - (npair - 2))
                            Pg.op("dve", lambda e, pt=pt, m0=m0: e.tensor_tensor(
                                out=PTt[pt][:].rearrange("p (k s q) -> p k s q", k=2, s=2),
                                in0=PTt[pt][:].rearrange("p (k s q) -> p k s q", k=2, s=2),
                                in1=maskt[:, m0:m0 + 2, :].unsqueeze(2).to_broadcast([128, 2, 2, 128]), op=ALU.mult),
                                [PTb[pt], maskb], [PTb[pt]])
                        mms = []
                        for kl in range(2):
                            kt = 2 * p + kl
                            for s_, ab in ((0, a0), (1, a1)):
                                c0 = (kl * 2 + s_) * 128
                                mms.append((FB_[ab][:, 0:257], PTt[pt][:, c0:c0 + 128], Vh[:, kt, :], kt == 0, kt == nkt - 1))
                        Pg.mm_group(mms, [PTb[pt], Vb.sub(2 * p, 2 * p + 2)], [FBb[a0], FBb[a1]])
                        if p + 2 < npair:
                            scores(p + 2)
                    o = j % 2
                    finalize(FB_[a0], FB_[a1], FBb[a0], FBb[a1], 0, 128, oat[o][:, :], oatb[o])
                    Pg.dma("pool", oatt_s[j * 128:(j + 1) * 128, h * HW:(h + 1) * HW], oat[o][:], [oatb[o]], [], oatsem[o])

                for sbi in range(2):
                    r0 = sbi * 32
                    Pg.op("pool", lambda e, sbi=sbi, h=h: [e.dma_start(out=Kc[:], in_=cache_k[sbi, :, h * HW:(h + 1) * HW].rearrange("(t p) c -> p t c", p=128))],
                          [], [Kcb], sem=Kcs)
                    Pg.op("pool", lambda e, sbi=sbi, h=h: [e.dma_start(out=Vc[:, :, 0:256], in_=cache_v[sbi, :, h * HW:(h + 1) * HW].rearrange("(t p) c -> p t c", p=128))],
                          [], [Vcb], sem=Vcs)
                    for t0 in range(0, PT, 4):
                        tb = state["tb"] % 2
                        state["tb"] += 1
                        t1 = min(PT, t0 + 4)
                        items = []
                        for t in range(t0, t1):
                            for s_ in range(2):
                                c0 = ((t - t0) * 2 + s_) * 128
                                items.append((TB_[tb][:, c0:c0 + 128], Kc[:, t, s_ * 128:(s_ + 1) * 128], 128))
                        Pg.transposes(items, ident, [Kcb, identb], [TBb[tb]])
                        n = t1 - t0
                        Pg.op("act", lambda e, tb=tb, t0=t0, n=n: e.activation(
                            out=KTc[:, :, t0 * 128:(t0 + n) * 128].rearrange("p s (t k) -> p t s k", k=128),
                            in_=TB_[tb][:, 0:n * 256].rearrange("p (t s k) -> p t s k", s=2, k=128), func=AF.Copy),
                            [TBb[tb]], [KTcb])
                    for s_ in range(2):
                        Pg.mm_group([(FB_[s_][:, t * 16:(t + 1) * 16], KTc[:, s_, t * 128:(t + 1) * 128], QTs[:, s_, r0:r0 + 16], True, True)
                                     for t in range(PT)], [KTcb, QTsb], [FBb[s_]])
                    Pg.mm_group([(FB_[4][r0:r0 + 16, s_ * 16:(s_ + 1) * 16], KTs[:, s_, r0:r0 + 16], QTs[:, s_, r0:r0 + 16], True, True)
                                 for s_ in range(2)], [KTsb, QTsb], [FBb[4]])
                    for s_ in range(2):
                        Pg.op("act", lambda e, s_=s_: e.activation(out=PTt[s_][:, 0:PT * 16], in_=FB_[s_][:, 0:PT * 16], func=AF.Exp, scale=scale),
                              [FBb[s_]], [PTb[s_]])
                    Pg.op("act", lambda e, r0=r0: e.activation(out=PTn[r0:r0 + 16, :], in_=FB_[4][r0:r0 + 16, 0:32], func=AF.Exp, scale=scale),
                          [FBb[4]], [PTnb])
                    for s_ in range(2):
                        mms = [(FB_[2 + s_][r0:r0 + 16, 0:257], PTt[s_][:, t * 16:(t + 1) * 16], Vc[:, t, :], t == 0, False) for t in range(PT)]
                        mms.append((FB_[2 + s_][r0:r0 + 16, 0:257], PTn[r0:r0 + 16, s_ * 16:(s_ + 1) * 16], Vs[r0:r0 + 16, :], False, True))
                        Pg.mm_group(mms, [PTb[s_], PTnb, Vcb, Vsb], [FBb[2 + s_]])
                    finalize(FB_[2], FB_[3], FBb[2], FBb[3], r0, r0 + 16, oats[r0:r0 + 16, :], oatsb)
                Pg.dma("pool", oatt_s[NJ * 128:NJ * 128 + SROWS, h * HW:(h + 1) * HW], oats[:], [oatsb], [], oatssem)
            Pg.flush()
            if stop == 1:
                Pg.flush()
                return nc

        GT = 3
        groups = [list(range(i, min(NOWN, i + GT))) for i in range(0, NOWN, GT)]
        with ExitStack() as ph:
            R1 = sb(ph, "p2_R1", [128, GT, KC, 128], BF16)
            R1b = Buf(GT)
            R1sem = [Pg.dsem(f"p2R1_{i}") for i in range(GT)]
            Mb = Buf(2 * GT)
            NWB = 5
            WBE = max(KC * CB, 2 * D)
            WB = [sb(ph, f"p2_WB{i}", [128, WBE], BF16) for i in range(NWB)]
            WBb = [Buf() for _ in range(NWB)]
            WBsem = [Pg.dsem(f"p2WB{i}") for i in range(NWB)]
            ssv = sb(ph, "p2_ssv", [128, GT, NCB], F32)
            ssvb = Buf()
            rsv = sb(ph, "p2_rsv", [128, GT], F32)
            rsvb = Buf()
            sgin = [sb(ph, f"p2_sgin{i}", [SROWS, CB], F32) for i in range(2)]
            sginb = [Buf() for _ in range(2)]
            sginsem = [Pg.dsem(f"p2sgin{i}") for i in range(2)]
            vs32b = Buf()
            wsT = sb(ph, "p2_wsT", [128, G, 128], F32)
            wsS = sb(ph, "p2_wsS", [SROWS, G, SROWS], F32)
            trl = sb(ph, "p2_tril", [128, 128], F32)
            trls = sb(ph, "p2_trils", [SROWS, SROWS], F32)
            wsb = Buf()
            Wm = sb(ph, "p2_Wm", [128, GT, G, 128], BF16)
            Wmb = Buf(GT)
            bst = sb(ph, "p2_bst", [128, G], F32)
            bss = sb(ph, "p2_bss", [SROWS, G], F32)
            bsb = Buf()
            gpk = sb(ph, "p2_gpk", [128, KC], F32)
            gpkb = Buf()
            NBC = 2
            bcA = [sb(ph, f"p2_bcA{i}", [128, CB], F32) for i in range(NBC)]
            bcB = [sb(ph, f"p2_bcB{i}", [128, CB], F32) for i in range(NBC)]
            bcS = [sb(ph, f"p2_bcS{i}", [128, CB], F32) for i in range(NBC)]
            bcb = [Buf() for _ in range(NBC)]
            bcsem = [Pg.dsem(f"p2bc{i}") for i in range(NBC)]
            oa = [sb(ph, f"p2_oa{i}", [128, CB], BF16) for i in range(2)]
            oab = [Buf() for _ in range(2)]
            oasem = [Pg.dsem(f"p2oa{i}") for i in range(2)]
            e1 = [sb(ph, f"p2_e1{i}", [128, CB], F32) for i in range(2)]
            e2 = [sb(ph, f"p2_e2{i}", [128, CB], F32) for i in range(2)]
            e3 = [sb(ph, f"p2_e3{i}", [128, CB], F32) for i in range(2)]
            e4 = [sb(ph, f"p2_e4{i}", [128, CB], F32) for i in range(2)]
            eb = [[Buf() for _ in range(4)] for _ in range(2)]
            sgst = [sb(ph, f"p2_sgst{i}", [SROWS, CB], F32) for i in range(2)]
            sgstb = [Buf() for _ in range(2)]
            sgsem = [Pg.dsem(f"p2sg{i}") for i in range(2)]
            junk = sb(ph, "p2_junk", [128, 512], BF16)
            junkb = Buf()
            hfs = [sb(ph, f"p2_hfs{i}", [128, D], BF16) for i in range(1)]
            hfsb = [Buf() for _ in range(1)]
            FBLK = 512 if F >= 512 else F
            NFC = FBLK // 128
            actT = [sb(ph, f"p2_actT{i}", [128, NFC, GT * 128], BF16) for i in range(2)]
            actTb = [Buf() for _ in range(2)]
            rl = [sb(ph, f"p2_rl{i}", [128, 512], F32) for i in range(2)]
            rlb = [Buf() for _ in range(2)]
            ysem = [Pg.dsem(f"p2y{i}") for i in range(GT)]

            csem = Pg.dsem("p2const")
            Pg.op("sp", lambda e: [e.dma_start(out=wsT[:].rearrange("p g t -> p (g t)"), in_=wsT_d),
                                   e.dma_start(out=wsS[:].rearrange("p g t -> p (g t)"), in_=wsS_d),
                                   e.dma_start(out=trl[:], in_=tril_d), e.dma_start(out=trls[:], in_=trils_d),
                                   e.dma_start(out=bst[:], in_=bs_t), e.dma_start(out=bss[:], in_=bs_s),
                                   e.dma_start(out=gpk[:], in_=nffn_pk)],
                  [], [wsb, bsb, gpkb], sem=csem, ninc=7)
            Pg.op("dve", lambda e: e.tensor_tensor(out=wsT[:], in0=wsT[:], in1=trl[:].unsqueeze(1).to_broadcast([128, G, 128]), op=ALU.mult),
                  [wsb], [wsb])
            Pg.op("dve", lambda e: e.tensor_tensor(out=wsS[:], in0=wsS[:], in1=trls[:].unsqueeze(1).to_broadcast([SROWS, G, SROWS]), op=ALU.mult),
                  [wsb], [wsb])

            w3 = w_in.rearrange("(k p) c -> p k c", p=128)
            wo3 = w_o.rearrange("(k p) c -> p k c", p=128)
            wu3 = w_up.rearrange("(k p) c -> p k c", p=128)
            wd3 = w_down.rearrange("(k p) c -> p k c", p=128)
            st2 = {"wb": 0, "fb": 0, "tb": 0, "bc": 0, "e": 0, "oa": 0, "sg": 0}

            def load_w(src3, c0, ncols):
                i = st2["wb"] % NWB
                st2["wb"] += 1
                assert not wpend[i], "weight slot overwritten before its consumers were generated"
                wpend[i] = True
                Pg.op("pool", lambda e, i=i: [e.dma_start(out=WB[i][:, 0:KC * ncols].rearrange("p (k c) -> p k c", c=ncols),
                                                          in_=src3[:, :, c0:c0 + ncols])], [], [WBb[i]], sem=WBsem[i])
                return i

            wpend = [False] * NWB

            def rel(*slots):
                for i in slots:
                    wpend[i] = False

            def wv(i, ncols):
                return WB[i][:, 0:KC * ncols].rearrange("p (k c) -> p k c", c=ncols)

            for gidx, tiles in enumerate(groups):
                ng = len(tiles)
                gs1 = ExitStack()
                Mbf = sb(gs1, f"p2_Mbf{gidx}", [128, GT, 2 * D], BF16)

                def vsr(gi, Mbf=Mbf):
                    return Mbf[:, gi, 0:D]

                def mrg(gi, Mbf=Mbf):
                    return Mbf[:, gi, D:2 * D]
                rws = [cfg.rows(t) for t in tiles]
                has_s = tiles[-1] == NOWN - 1
                for gi, t in enumerate(tiles):
                    Pg.dma("sp", R1[:, gi].rearrange("p k t -> p (k t)"), hT_own[t], [], [R1b.sub(gi)], R1sem[gi])
                Pg.op("pool", lambda e: e.memset(ssv[:], 0.0), [], [ssvb])
                import os as _os
                _skip = _os.environ.get("MK_SKIP", "")
                nxt = load_w(w3, 4 * D, CB)
                for cb in range(NCB if "A" not in _skip else 0):
                    wi = nxt
                    if cb + 1 < NCB:
                        nxt = load_w(w3, 4 * D + (cb + 1) * CB, CB)
                    for gi, t in enumerate(tiles):
                        rows = rws[gi]
                        fb = st2["fb"] % 4
                        st2["fb"] += 1
                        Pg.mm_group([(FB_[fb][:rows, 0:CB], R1[:, gi, kc, :rows], wv(wi, CB)[:, kc, :], kc == 0, kc == KC - 1) for kc in range(KC)],
                                    [R1b.sub(gi), WBb[wi]], [FBb[fb]])
                        if "D" not in _skip:
                          Pg.op("act", lambda e, fb=fb, gi=gi, cb=cb, rows=rows: e.activation(
                            out=junk[:rows, 0:CB], in_=FB_[fb][:rows, 0:CB], func=AF.Square, accum_out=ssv[:rows, gi, cb:cb + 1]),
                            [FBb[fb]], [junkb, ssvb])
                        if "B" not in _skip:
                          Pg.op("act", lambda e, fb=fb, gi=gi, cb=cb, rows=rows, vsr=vsr: e.activation(out=vsr(gi)[:rows, cb * CB:(cb + 1) * CB], in_=FB_[fb][:rows, 0:CB], func=AF.Copy),
                              [FBb[fb]], [Mb.sub(2 * gi)])
                        if has_s and gi == ng - 1 and "C" not in _skip:
                            si = st2["sg"] % 2
                            st2["sg"] += 1
                            Pg.op("act", lambda e, fb=fb, si=si, rows=rows: e.activation(out=sgst[si][:, :], in_=FB_[fb][:rows, 0:CB], func=AF.Copy),
                                  [FBb[fb]], [sgstb[si]])
                            Pg.dma("sp", vs32_s[:, cb * CB:(cb + 1) * CB], sgst[si][:, :], [sgstb[si]], [vs32b], sgsem[si])
                    rel(wi)
                if stop == 2:
                    Pg.flush(); Pg.flush()
                    gs1.close()
                    return nc
                Pg.op("dve", lambda e: e.tensor_reduce(out=rsv[:], in_=ssv[:], axis=AX.X, op=ALU.add), [ssvb], [rsvb])
                rstd_ops(rsv[:], rsv[:], rsvb, rsvb, D)
                for gi, t in enumerate(tiles):
                    rows = rws[gi]
                    if rows == 128:
                        Pg.op("dve", lambda e, gi=gi: e.tensor_scalar(out=Wm[:, gi], in0=wsT[:], scalar1=rsv[:, gi:gi + 1], scalar2=None, op0=ALU.mult),
                              [wsb, rsvb], [Wmb.sub(gi)])
                    else:
                        Pg.op("dve", lambda e, gi=gi, rows=rows: e.tensor_scalar(out=Wm[:rows, gi, :, :rows], in0=wsS[:], scalar1=rsv[:rows, gi:gi + 1],
                                                                                 scalar2=None, op0=ALU.mult),
                              [wsb, rsvb], [Wmb.sub(gi)])
                if stop == 3:
                    Pg.flush(); Pg.flush()
                    gs1.close()
                    return nc
                def load_cb1(cb):
                    wu_ = load_w(w3, 3 * D + cb * CB, CB)
                    wa_ = load_w(w3, 5 * D + cb * CB, CB)
                    return (wu_, wa_)

                def load_cb2(cb):
                    wb_ = load_w(w3, 6 * D + cb * CB, CB)
                    bi = st2["bc"] % NBC
                    st2["bc"] += 1
                    Pg.op("sp", lambda e, bi=bi, cb=cb: [e.dma_start(out=bcA[bi][:], in_=bgate_bc[:, cb * CB:(cb + 1) * CB]),
                                                        e.dma_start(out=bcB[bi][:], in_=bgate_bc[:, D + cb * CB: D + (cb + 1) * CB]),
                                                        e.dma_start(out=bcS[bi][:], in_=sgu_bc[:, cb * CB:(cb + 1) * CB])],
                          [], [bcb[bi]], sem=bcsem[bi], ninc=3)
                    return (wb_, bi)

                nxt1 = load_cb1(0)
                nxt2 = load_cb2(0)
                for cb in range(NCB):
                    wu_, wa_ = nxt1
                    wb_, bi = nxt2
                    if cb + 1 < NCB:
                        nxt1 = load_cb1(cb + 1)
                    segs = []
                    c = cb * CB
                    while c < (cb + 1) * CB:
                        g = c // GD
                        ce = min((g + 1) * GD, (cb + 1) * CB)
                        segs.append((g, c - cb * CB, ce - cb * CB))
                        c = ce
                    for gi, t in enumerate(tiles):
                        rows = rws[gi]
                        is_s = rows < 128
                        fa = (st2["fb"] % 2) * 2
                        st2["fb"] += 1
                        f1, f2 = FB_[fa], FB_[fa + 1]
                        f1b, f2b = FBb[fa], FBb[fa + 1]
                        ei = st2["e"] % 2
                        st2["e"] += 1
                        oi = st2["oa"] % 2
                        st2["oa"] += 1
                        Pg.dma("sp", oa[oi][:rows, :], oatt_s[t * 128:t * 128 + rows, cb * CB:(cb + 1) * CB], [], [oab[oi]], oasem[oi])
                        mms = [(f1[:rows, 0:CB], R1[:, gi, kc, :rows], wv(wu_, CB)[:, kc, :], kc == 0, kc == KC - 1) for kc in range(KC)]
                        mms += [(f1[:rows, CB:2 * CB], R1[:, gi, kc, :rows], wv(wa_, CB)[:, kc, :], kc == 0, kc == KC - 1) for kc in range(KC)]
                        Pg.mm_group(mms, [R1b.sub(gi), WBb[wu_], WBb[wa_]], [f1b])
                        mms = [(f2[:rows, 0:CB], R1[:, gi, kc, :rows], wv(wb_, CB)[:, kc, :], kc == 0, kc == KC - 1) for kc in range(KC)]
                        for (g, s0, s1) in segs:
                            mms.append((f2[:rows, CB + s0:CB + s1], Wm[:rows, gi, g, :rows], vsr(gi)[:rows, cb * CB + s0:cb * CB + s1], True, True))
                        Pg.mm_group(mms, [R1b.sub(gi), WBb[wb_], Wmb.sub(gi), Mb.sub(2 * gi)], [f2b])
                        Pg.op("dve", lambda e, ei=ei, bi=bi, rows=rows, f1=f1: e.tensor_tensor(out=e1[ei][:rows, :], in0=f1[:rows, CB:2 * CB], in1=bcA[bi][:rows, :], op=ALU.add),
                              [f1b, bcb[bi]], [eb[ei][0]])
                        Pg.op("act", lambda e, ei=ei, rows=rows: e.activation(out=e1[ei][:rows, :], in_=e1[ei][:rows, :], func=AF.Sigmoid), [eb[ei][0]], [eb[ei][0]])
                        Pg.op("dve", lambda e, ei=ei, bi=bi, rows=rows, f2=f2: e.tensor_tensor(out=e2[ei][:rows, :], in0=f2[:rows, 0:CB], in1=bcB[bi][:rows, :], op=ALU.add),
                              [f2b, bcb[bi]], [eb[ei][1]])
                        Pg.op("act", lambda e, ei=ei, rows=rows: e.activation(out=e2[ei][:rows, :], in_=e2[ei][:rows, :], func=AF.Sigmoid), [eb[ei][1]], [eb[ei][1]])
                        Pg.op("dve", lambda e, ei=ei, bi=bi, rows=rows, f2=f2: e.tensor_tensor(out=e3[ei][:rows, :], in0=f2[:rows, CB:2 * CB], in1=bcS[bi][:rows, :], op=ALU.mult),
                              [f2b, bcb[bi]], [eb[ei][2]])
                        bsrc = bss if is_s else bst
                        for (g, s0, s1) in segs:
                            Pg.op("dve", lambda e, ei=ei, rows=rows, f1=f1, g=g, s0=s0, s1=s1, bsrc=bsrc: e.scalar_tensor_tensor(
                                out=e3[ei][:rows, s0:s1], in0=e3[ei][:rows, s0:s1], scalar=bsrc[:rows, g:g + 1], in1=f1[:rows, s0:s1],
                                op0=ALU.add, op1=ALU.mult), [f1b, bsb, eb[ei][2]], [eb[ei][2]])
                        Pg.op("pool", lambda e, ei=ei, oi=oi, rows=rows: e.tensor_tensor(out=e1[ei][:rows, :], in0=e1[ei][:rows, :], in1=oa[oi][:rows, :], op=ALU.mult),
                              [eb[ei][0], oab[oi]], [eb[ei][0]])
                        Pg.op("dve", lambda e, ei=ei, rows=rows: e.tensor_tensor(out=e2[ei][:rows, :], in0=e2[ei][:rows, :], in1=e3[ei][:rows, :], op=ALU.mult),
                              [eb[ei][1], eb[ei][2]], [eb[ei][1]])
                        Pg.op("dve", lambda e, ei=ei, gi=gi, cb=cb, rows=rows: e.tensor_tensor(out=mrg(gi)[:rows, cb * CB:(cb + 1) * CB], in0=e1[ei][:rows, :], in1=e2[ei][:rows, :], op=ALU.add),
                              [eb[ei][0], eb[ei][1]], [Mb.sub(2 * gi + 1)])
                        if is_s:
                            si = st2["sg"] % 2
                            st2["sg"] += 1
                            Pg.dma("sp", sgin[si][:, :], vs32_s[:, cb * CB:(cb + 1) * CB], [vs32b], [sginb[si]], sginsem[si])
                            Pg.op("dve", lambda e, si=si, gi=gi, bi=bi, cb=cb, rows=rows: e.scalar_tensor_tensor(
                                out=sgst[si][:, :], in0=sgin[si][:, :], scalar=rsv[:rows, gi:gi + 1], in1=bcS[bi][:rows, :],
                                op0=ALU.mult, op1=ALU.mult), [sginb[si], rsvb, bcb[bi]], [sgstb[si]])
                            Pg.dma("sp", sgu_s[:, cb * CB:(cb + 1) * CB], sgst[si][:, :], [sgstb[si]], [], sgsem[si])
                    rel(wu_, wa_, wb_)
                    if cb + 1 < NCB:
                        nxt2 = load_cb2(cb + 1)
                if stop == 4:
                    Pg.flush(); Pg.flush()
                    gs1.close()
                    return nc
                for gi, t in enumerate(tiles):
                    rows = rws[gi]
                    for g0 in range(0, KC, 8):
                        g1 = min(KC, g0 + 8)
                        tb = st2["tb"] % 2
                        st2["tb"] += 1
                        items = [(TB_[tb][:, (kc - g0) * 128:(kc - g0) * 128 + rows], mrg(gi)[:rows, kc * 128:(kc + 1) * 128], rows) for kc in range(g0, g1)]
                        Pg.transposes(items, ident, [Mb.sub(2 * gi + 1), identb], [TBb[tb]])
                        Pg.op("act", lambda e, tb=tb, gi=gi, g0=g0, g1=g1, rows=rows: e.activation(
                            out=R1[:, gi, g0:g1, :rows], in_=TB_[tb][:, 0:(g1 - g0) * 128].rearrange("p (k t) -> p k t", t=128)[:, :, :rows], func=AF.Copy),
                            [TBb[tb]], [R1b.sub(gi)])
                Pg.flush()
                gs1.close()
                gs2 = ExitStack()
                M = sb(gs2, f"p2_Y{gidx}", [128, GT, D], F32)
                for gi, t in enumerate(tiles):
                    rows = rws[gi]
                    Pg.dma("sp", M[:rows, gi, :], x_own[t * 128:t * 128 + rows, :], [], [Mb.sub(2 * gi, 2 * gi + 2)], ysem[gi])
                Pg.op("pool", lambda e: e.memset(ssv[:], 0.0), [], [ssvb])
                nxt = load_w(wo3, 0, CB)
                for cb in range(NCB):
                    wi = nxt
                    if cb + 1 < NCB:
                        nxt = load_w(wo3, (cb + 1) * CB, CB)
                    for gi, t in enumerate(tiles):
                        rows = rws[gi]
                        fb = st2["fb"] % 4
                        st2["fb"] += 1
                        Pg.mm_group([(FB_[fb][:rows, 0:CB], R1[:, gi, kc, :rows], wv(wi, CB)[:, kc, :], kc == 0, kc == KC - 1) for kc in range(KC)],
                                    [R1b.sub(gi), WBb[wi]], [FBb[fb]])
                        Pg.op("dve", lambda e, fb=fb, gi=gi, cb=cb, rows=rows: e.tensor_tensor(
                            out=M[:rows, gi, cb * CB:(cb + 1) * CB], in0=FB_[fb][:rows, 0:CB], in1=M[:rows, gi, cb * CB:(cb + 1) * CB], op=ALU.add),
                            [FBb[fb]], [Mb.sub(2 * gi, 2 * gi + 2)])
                        Pg.op("act", lambda e, gi=gi, cb=cb, rows=rows: e.activation(
                            out=junk[:rows, 0:CB], in_=M[:rows, gi, cb * CB:(cb + 1) * CB], func=AF.Square, accum_out=ssv[:rows, gi, cb:cb + 1]),
                            [Mb.sub(2 * gi, 2 * gi + 2)], [junkb, ssvb])
                    rel(wi)
                Pg.op("dve", lambda e: e.tensor_reduce(out=rsv[:], in_=ssv[:], axis=AX.X, op=ALU.add), [ssvb], [rsvb])
                rstd_ops(rsv[:], rsv[:], rsvb, rsvb, D)
                ntok = ng * 128
                R1f = R1[:].rearrange("p g k t -> p (g k t)")[:, 0:KC * ntok].rearrange("p (k n) -> p k n", n=ntok)
                for gi, t in enumerate(tiles):
                    rows = rws[gi]
                    hs = 0
                    Pg.op("dve", lambda e, gi=gi, hs=hs, rows=rows: e.tensor_scalar(out=hfs[hs][:rows, :], in0=M[:rows, gi, :], scalar1=rsv[:rows, gi:gi + 1],
                                                                                   scalar2=None, op0=ALU.mult),
                          [Mb.sub(2 * gi, 2 * gi + 2), rsvb], [hfsb[hs]])
                    for g0 in range(0, KC, 8):
                        g1 = min(KC, g0 + 8)
                        tb = st2["tb"] % 2
                        st2["tb"] += 1
                        items = [(TB_[tb][:, (kc - g0) * 128:(kc - g0) * 128 + rows], hfs[hs][:rows, kc * 128:(kc + 1) * 128], rows) for kc in range(g0, g1)]
                        Pg.transposes(items, ident, [hfsb[hs], identb], [TBb[tb]])
                        Pg.op("dve", lambda e, tb=tb, gi=gi, g0=g0, g1=g1, rows=rows: e.tensor_tensor(
                            out=R1f[:, g0:g1, gi * 128:gi * 128 + rows],
                            in0=TB_[tb][:, 0:(g1 - g0) * 128].rearrange("p (k t) -> p k t", t=128)[:, :, :rows],
                            in1=gpk[:, g0:g1].unsqueeze(2).to_broadcast([128, g1 - g0, rows]), op=ALU.mult),
                            [TBb[tb], gpkb], [R1b])
                if stop == 5:
                    Pg.flush(); Pg.flush()
                    gs2.close()
                    return nc
                ntk = (ng - 1) * 128 + rws[-1]
                mv_tiles = [(m0, min(512, ntk - m0)) for m0 in range(0, ntk, 512)]
                NFB = F // FBLK
                UPC = min(CB, FBLK)
                NUP = FBLK // UPC
                DNC = min(2, NFC)
                NDN = NFC // DNC

                def load_ups(fbk):
                    return [load_w(wu3, fbk * FBLK + u * UPC, UPC) for u in range(NUP)]

                def load_dns(fbk):
                    dns = []
                    for d_ in range(NDN):
                        i = st2["wb"] % NWB
                        st2["wb"] += 1
                        assert not wpend[i], "weight slot overwritten before its consumers were generated"
                        wpend[i] = True
                        c0 = fbk * NFC + d_ * DNC
                        Pg.op("pool", lambda e, i=i, c0=c0: [e.dma_start(out=WB[i][:, 0:DNC * D].rearrange("p (k c) -> p k c", c=D), in_=wd3[:, c0:c0 + DNC, :])],
                              [], [WBb[i]], sem=WBsem[i])
                        dns.append(i)
                    return dns

                nups = load_ups(0)
                ndns = load_dns(0)
                for fbk in range(NFB):
                    ups, dns = nups, ndns
                    ai = fbk % 2
                    for fc in range(NFC):
                        u = (fc * 128) // UPC
                        uo = (fc * 128) % UPC
                        for (m0, mn) in mv_tiles:
                            fb = st2["fb"] % 2
                            st2["fb"] += 1
                            Pg.mm_group([(FB_[fb][:, 0:mn], wv(ups[u], UPC)[:, kc, uo:uo + 128], R1f[:, kc, m0:m0 + mn], kc == 0, kc == KC - 1) for kc in range(KC)],
                                        [R1b, WBb[ups[u]]], [FBb[fb]])
                            ri = st2["e"] % 2
                            st2["e"] += 1
                            Pg.op("act", lambda e, fb=fb, ri=ri, mn=mn: e.activation(out=rl[ri][:, 0:mn], in_=FB_[fb][:, 0:mn], func=AF.Relu),
                                  [FBb[fb]], [rlb[ri]])
                            Pg.op("pool", lambda e, ri=ri, ai=ai, fc=fc, m0=m0, mn=mn: e.tensor_tensor(out=actT[ai][:, fc, m0:m0 + mn], in0=rl[ri][:, 0:mn], in1=rl[ri][:, 0:mn], op=ALU.mult),
                                  [rlb[ri]], [actTb[ai]])
                    rel(*ups)
                    if fbk + 1 < NFB:
                        nups = load_ups(fbk + 1)
                    for gi, t in enumerate(tiles):
                        rows = rws[gi]
                        for db in range(0, D, 512):
                            dn = min(512, D - db)
                            fb = 2 + st2["tb"] % 4
                            st2["tb"] += 1
                            mms = []
                            for fc in range(NFC):
                                mms.append((FB_[fb][:rows, 0:dn], actT[ai][:, fc, gi * 128:gi * 128 + rows],
                                            WB[dns[fc // DNC]][:, 0:DNC * D].rearrange("p (k c) -> p k c", c=D)[:, fc % DNC, db:db + dn], fc == 0, fc == NFC - 1))
                            Pg.mm_group(mms, [actTb[ai]] + [WBb[i] for i in dns], [FBb[fb]])
                            Pg.op("dve", lambda e, fb=fb, gi=gi, db=db, dn=dn, rows=rows: e.tensor_tensor(
                                out=M[:rows, gi, db:db + dn], in0=FB_[fb][:rows, 0:dn], in1=M[:rows, gi, db:db + dn], op=ALU.add),
                                [FBb[fb]], [Mb.sub(2 * gi, 2 * gi + 2)])
                    rel(*dns)
                    if fbk + 1 < NFB:
                        ndns = load_dns(fbk + 1)
                for gi, t in enumerate(tiles):
                    rows = rws[gi]
                    Pg.dma("sp", y_own[t * 128:t * 128 + rows, :], M[:rows, gi, :], [Mb.sub(2 * gi, 2 * gi + 2)], [], ysem[gi])
                Pg.flush()
                gs2.close()
            Pg.flush()
    return nc


def rope_tables(pos):
    half = 64
    inv = (ROPE_THETA ** (-np.arange(half, dtype=np.float32) / half)).astype(np.float32)
    ang = pos.astype(np.float32)[:, None] * inv[None, :]
    cos = np.cos(ang).astype(np.float32)
    sin = np.sin(ang).astype(np.float32)
    return np.concatenate([cos, cos, -sin, sin], axis=1).astype(np.float32)


def make_in_maps(cfg, inp):
    D, H, T, P, G = cfg.D, cfg.H, cfg.T, cfg.P, cfg.G
    NJ, SROWS, OWNROWS, KC = cfg.NJ, cfg.SROWS, cfg.OWNROWS, cfg.KC
    f = np.float32
    a = {k: np.asarray(v) for k, v in inp.items()}
    HWt = H * 256
    ident = np.eye(128, dtype=f)
    sidx = np.arange(128)
    tril = (sidx[:, None] <= sidx[None, :]).astype(f)
    trils = np.zeros((64, 64), f)
    for b0 in (0, 32):
        trils[b0:b0 + 16, b0:b0 + 16] = tril[:16, :16]
    w_s = a["w_s"][0]
    wsT = np.ascontiguousarray(np.transpose(w_s, (2, 0, 1))).reshape(128, G * 128)
    wsS = np.zeros((64, G, 64), f)
    for b0 in (0, 32):
        wsS[b0:b0 + 16, :, b0:b0 + 16] = np.transpose(w_s[:, :16, :16], (2, 0, 1))
    wsS = wsS.reshape(64, G * 64)
    bs_t = np.ascontiguousarray(a["b_s"][0].T)
    bs_s = np.zeros((64, G), f)
    for b0 in (0, 32):
        bs_s[b0:b0 + 16] = bs_t[:16]
    bc = lambda v: np.ascontiguousarray(np.broadcast_to(np.asarray(v, f)[None, :], (128, v.shape[0])))
    lam4 = np.concatenate([bc(a["lambda_q1"][0]), bc(a["lambda_k1"][0]), bc(a["lambda_q2"][0]), bc(a["lambda_k2"][0])], axis=1)
    common = {
        "ident": ident, "tril": tril, "trils": trils,
        "w_in": np.ascontiguousarray(a["w_in"][0]), "bgate_bc": bc(a["b_gate"][0]),
        "qn_bc": bc(a["q_norm"][0]), "kn_bc": bc(a["k_norm"][0]), "lam4": lam4,
        "subln_bc": bc(a["subln"][0]), "sgu_bc": bc(a["sgu_norm"][0]),
        "wsT": wsT, "wsS": wsS, "bs_t": bs_t, "bs_s": bs_s,
        "w_o": np.ascontiguousarray(a["w_o"][0]),
        "nmix_pk": np.ascontiguousarray(a["norm_mix"][0].reshape(KC, 128).T),
        "nffn_pk": np.ascontiguousarray(a["norm_ffn"][0].reshape(KC, 128).T),
        "w_up": np.ascontiguousarray(a["w_up"][0]), "w_down": np.ascontiguousarray(a["w_down"][0]),
        "cs_all": rope_tables(np.arange(T)),
    }
    spos = P + np.arange(cfg.ST)
    in_maps = []
    for c in range(N_CORES):
        b, r = c // 4, c % 4
        m = dict(common)
        m["x_all"] = np.ascontiguousarray(a["x_prompt"][b])
        xo = np.zeros((OWNROWS, D), f)
        pos = np.zeros((OWNROWS,), np.int64)
        for j in range(NJ):
            t = 4 * j + r
            xo[j * 128:(j + 1) * 128] = a["x_prompt"][b, t * 128:(t + 1) * 128]
            pos[j * 128:(j + 1) * 128] = np.arange(t * 128, (t + 1) * 128)
        for s in range(2):
            xo[NJ * 128 + 32 * s: NJ * 128 + 32 * s + 16] = a["x_sample"][2 * c + s]
            pos[NJ * 128 + 32 * s: NJ * 128 + 32 * s + 16] = spos
        m["x_own"] = xo
        m["cs_own"] = rope_tables(pos)
        m["cache_k"] = np.ascontiguousarray(a["cache_k_attn"][0, 2 * c:2 * c + 2].reshape(2, P, HWt))
        m["cache_v"] = np.ascontiguousarray(a["cache_v_attn"][0, 2 * c:2 * c + 2].reshape(2, P, HWt))
        mk = np.zeros((4, 128, 128), f)
        kk = np.arange(128)
        for mt in range(4):
            if mt < r:
                mk[mt] = 1.0
            elif mt == r:
                mk[mt] = ((kk[:, None] // 64) <= (kk[None, :] // 64)).astype(f)
        m["masks"] = np.ascontiguousarray(np.transpose(mk, (1, 0, 2))).reshape(128, 4 * 128)
        in_maps.append(m)
    return in_maps


def assemble(cfg, res):
    D, H, T = cfg.D, cfg.H, cfg.T
    NJ, SB, ST, NB = cfg.NJ, cfg.SB, cfg.ST, cfg.NB
    f = np.float32
    y_p = np.zeros((NB, T, D), f)
    y_s = np.zeros((SB, ST, D), f)
    k_p = np.zeros((1, NB, T, H, 256), f)
    v_p = np.zeros((1, NB, T, H, 256), f)
    k_s = np.zeros((1, SB, ST, H, 256), f)
    v_s = np.zeros((1, SB, ST, H, 256), f)
    sg = np.zeros((1, SB, ST, D), f)
    for c in range(N_CORES):
        b, r = c // 4, c % 4
        o = res[c]
        yo = np.asarray(o["y_own"])
        for j in range(NJ):
            t = 4 * j + r
            y_p[b, t * 128:(t + 1) * 128] = yo[j * 128:(j + 1) * 128]
        for s in range(2):
            r0 = NJ * 128 + 32 * s
            y_s[2 * c + s] = yo[r0:r0 + 16]
            k_s[0, 2 * c + s] = np.asarray(o["k_s"])[32 * s:32 * s + 16].reshape(ST, H, 256)
            v_s[0, 2 * c + s] = np.asarray(o["v_s"])[32 * s:32 * s + 16].reshape(ST, H, 256)
            sg[0, 2 * c + s] = np.asarray(o["sgu_s"])[32 * s:32 * s + 16]
        if r == 0:
            k_p[0, b] = np.asarray(o["k_all"]).reshape(T, H, 256)
            v_p[0, b] = np.asarray(o["v_all"]).reshape(T, H, 256)
    return (y_p, y_s, k_p, v_p, k_s, v_s, sg)


def run(cfg, inputs, stop=9):
    nc = build_program(cfg, stop)
    in_maps = make_in_maps(cfg, inputs)
    res = run_bass_kernel_spmd(nc, in_maps, core_ids=list(range(N_CORES)))
    return assemble(cfg, res.results)


def kernel(**inputs):
    cfg = Cfg()
    return run(cfg, inputs)
```

```python
import math
from contextlib import ExitStack

import numpy as np
import concourse.bass as bass
import concourse.mybir as mybir
from concourse.bass_utils import run_bass_kernel_spmd

F32 = mybir.dt.float32
BF16 = mybir.dt.bfloat16
AF = mybir.ActivationFunctionType
ALU = mybir.AluOpType
AX = mybir.AxisListType

EPS = 1e-6
ROPE_THETA = 10000.0
N_CORES = 8


class Cfg:
    def __init__(self, D=4096, H=16, T=4096, P=4096, F=16384, G=8, SB=16, ST=16, NB=2):
        self.D, self.H, self.T, self.P, self.F, self.G, self.SB, self.ST, self.NB = D, H, T, P, F, G, SB, ST, NB
        self.KC = D // 128
        self.NT = T // 128
        self.NJ = self.NT // 4
        self.NOWN = self.NJ + 1
        self.SROWS = 64
        self.OWNROWS = self.NJ * 128 + self.SROWS
        self.PT = P // 128
        self.CB = 256
        self.NCB = D // self.CB
        self.GD = D // G
        self.lambda_init = 0.8 - 0.6 * math.exp(-0.3 * 0)

    def rows(self, ti):
        return 128 if ti < self.NJ else self.SROWS


class Cell:
    __slots__ = ("w", "r")

    def __init__(self):
        self.w = None
        self.r = {}


class Buf:
    def __init__(self, ncells=1, cells=None):
        self.cells = cells if cells is not None else [Cell() for _ in range(ncells)]

    def sub(self, i, j=None):
        return Buf(cells=self.cells[i:(i + 1 if j is None else j)])


ENGS = ("pe", "act", "dve", "pool", "sp")


class Prog:
    def __init__(self, nc, stack):
        self.nc = nc
        self.stack = stack
        self.q = {e: [] for e in ENGS}
        self.sems = {}
        self.count = {}
        self.waited = {e: {} for e in ENGS}
        for e in ENGS:
            self._sem(e)
        import os as _os
        for i in range(int(_os.environ.get("MK_DUMMY", "0"))):
            self._sem(f"dummy{i}")

    def _sem(self, key):
        if key not in self.sems:
            self.sems[key] = self.stack.enter_context(self.nc.semaphore("s_" + str(key)))
            self.count[key] = 0
        return key

    def dsem(self, name):
        return self._sem("d_" + name)

    def op(self, eng, fn, reads=(), writes=(), sem=None, ninc=1):
        needs = {}

        def add(ev):
            if ev is not None and needs.get(ev[0], 0) < ev[1]:
                needs[ev[0]] = ev[1]

        wcells = set()
        for b in writes:
            for c in b.cells:
                wcells.add(id(c))
                add(c.w)
                for k, v in c.r.items():
                    add((k, v))
        for b in reads:
            for c in b.cells:
                add(c.w)
        wt = self.waited[eng]
        waits = [(k, v) for k, v in needs.items() if wt.get(k, 0) < v]
        for k, v in waits:
            wt[k] = v
        if sem is None:
            key, amt = eng, 1
        else:
            key, amt = sem, 16 * ninc
        self.count[key] += amt
        val = self.count[key]
        for b in writes:
            for c in b.cells:
                c.w = (key, val)
                c.r = {}
        for b in reads:
            for c in b.cells:
                if id(c) not in wcells:
                    c.r[key] = val
        self.q[eng].append((waits, fn, key, sem is None))

    def flush(self):
        nc = self.nc
        attr = {"pe": "tensor", "act": "scalar", "dve": "vector", "pool": "gpsimd", "sp": "sync"}
        with nc.Block() as block:
            for name in ENGS:
                ops = self.q[name]

                def body(e, ops=ops):
                    for waits, fn, key, own in ops:
                        for k, v in waits:
                            e.wait_ge(self.sems[k], v)
                        if fn is None:
                            continue
                        r = fn(e)
                        if own:
                            if isinstance(r, (list, tuple)):
                                r = r[-1]
                            r.then_inc(self.sems[key], 1)
                        else:
                            for ins in r:
                                ins.then_inc(self.sems[key], 16)

                getattr(block, attr[name])(body)
        self.q = {e: [] for e in ENGS}
        for e in ENGS:
            waits = [(k, v) for k, v in self.count.items() if v > 0 and self.waited[e].get(k, 0) < v]
            for k, v in waits:
                self.waited[e][k] = v
            self.q[e].append((waits, None, None, False))

    def dma(self, q, out, in_, reads, writes, sem):
        self.op(q, lambda e: [e.dma_start(out=out, in_=in_)], reads, writes, sem=sem)

    def mm_group(self, mms, reads, writes):
        def fn(e, mms=mms):
            last = None
            for (o, l, r, st, sp) in mms:
                last = e.matmul(o, lhsT=l, rhs=r, start=st, stop=sp)
            return last
        self.op("pe", fn, reads, writes)

    def transposes(self, items, ident, reads, writes):
        def fn(e, items=items):
            last = None
            for (o, i, n) in items:
                last = e.transpose(o, i, ident[:n, :n])
            return last
        self.op("pe", fn, reads, writes)


def build_program(cfg, stop=9):
    D, H, T, P, F, G = cfg.D, cfg.H, cfg.T, cfg.P, cfg.F, cfg.G
    KC, NT, NJ, NOWN, SROWS, OWNROWS, PT = cfg.KC, cfg.NT, cfg.NJ, cfg.NOWN, cfg.SROWS, cfg.OWNROWS, cfg.PT
    CB, NCB, GD = cfg.CB, cfg.NCB, cfg.GD
    HW = 256
    scale = 128 ** -0.5

    nc = bass.Bass("TRN2", target_bir_lowering=False)

    def din(name, shape, dt=F32):
        return nc.dram_tensor(name, list(shape), dt, kind="ExternalInput").ap()

    def dout(name, shape, dt=F32):
        return nc.dram_tensor(name, list(shape), dt, kind="ExternalOutput").ap()

    x_all = din("x_all", [T, D])
    x_own = din("x_own", [OWNROWS, D])
    cache_k = din("cache_k", [2, P, H * HW])
    cache_v = din("cache_v", [2, P, H * HW])
    cs_all = din("cs_all", [T, 256])
    cs_own = din("cs_own", [OWNROWS, 256])
    masks_d = din("masks", [128, 4 * 128])
    ident_d = din("ident", [128, 128])
    tril_d = din("tril", [128, 128])
    trils_d = din("trils", [64, 64])
    w_in = din("w_in", [D, 7 * D])
    bgate_bc = din("bgate_bc", [128, 2 * D])
    qn_bc = din("qn_bc", [128, 128])
    kn_bc = din("kn_bc", [128, 128])
    lam_d = din("lam4", [128, 4 * 128])
    subln_bc = din("subln_bc", [128, 256])
    sgu_bc = din("sgu_bc", [128, D])
    wsT_d = din("wsT", [128, G * 128])
    wsS_d = din("wsS", [64, G * 64])
    bs_t = din("bs_t", [128, G])
    bs_s = din("bs_s", [64, G])
    w_o = din("w_o", [D, D])
    nmix_pk = din("nmix_pk", [128, KC])
    nffn_pk = din("nffn_pk", [128, KC])
    w_up = din("w_up", [D, F])
    w_down = din("w_down", [F, D])

    y_own = dout("y_own", [OWNROWS, D])
    k_all = dout("k_all", [T, H * HW])
    v_all = dout("v_all", [T, H * HW])
    k_s = dout("k_s", [SROWS, H * HW])
    v_s = dout("v_s", [SROWS, H * HW])
    sgu_s = dout("sgu_s", [SROWS, D])

    hT_all = nc.dram_tensor("hT_all", [NT, 128, KC * 128], BF16).ap()
    hT_own = nc.dram_tensor("hT_own", [NOWN, 128, KC * 128], BF16).ap()
    oatt_s = nc.dram_tensor("oatt_s", [OWNROWS, D], BF16).ap()
    vs32_s = nc.dram_tensor("vs32_s", [SROWS, D], F32).ap()
    x1_s = nc.dram_tensor("x1_s", [OWNROWS, D], F32).ap()
    hfT_s = nc.dram_tensor("hfT_s", [NOWN, 128, KC * 128], BF16).ap()

    with ExitStack() as top:
        Pg = Prog(nc, top)
        x1b = Buf()

        def sb(stack, name, shape, dt):
            return stack.enter_context(nc.sbuf_tensor(name, list(shape), dt))

        def ps(stack, name, shape, dt):
            return stack.enter_context(nc.psum_tensor(name, list(shape), dt))

        FB_ = [ps(top, f"F{i}", [128, 512], F32) for i in range(6)]
        FBb = [Buf() for _ in range(6)]
        TB_ = [ps(top, f"T{i}", [128, 1024], BF16) for i in range(2)]
        TBb = [Buf() for _ in range(2)]

        ident = sb(top, "ident_sb", [128, 128], BF16)
        identb = Buf()
        Pg.dma("pool", ident[:], ident_d, [], [identb], Pg.dsem("ident"))
        epsc = sb(top, "epsc", [128, 1], F32)
        epsb = Buf()
        Pg.op("pool", lambda e: e.memset(epsc[:], EPS), [], [epsb])

        def rstd_ops(ss_ap, out_ap, ssb, outb, n, eng="act"):
            Pg.op("act", lambda e: e.activation(out=out_ap, in_=ss_ap, func=AF.Ln, scale=1.0 / n, bias=epsc[:ss_ap.shape[0], :]), [ssb, epsb], [outb])
            Pg.op("act", lambda e: e.activation(out=out_ap, in_=out_ap, func=AF.Exp, scale=-0.5), [outb], [outb])

        with ExitStack() as ph:
            xs = [sb(ph, f"p0_xs{i}", [128, D], F32) for i in range(2)]
            xsb = [Buf() for _ in range(2)]
            xsem = [Pg.dsem(f"p0xs{i}") for i in range(2)]
            junk = sb(ph, "p0_junk", [128, D], BF16)
            junkb = Buf()
            hbf = [sb(ph, f"p0_hbf{i}", [128, D], BF16) for i in range(2)]
            hbfb = [Buf() for _ in range(2)]
            hst = [sb(ph, f"p0_hst{i}", [128, KC, 128], BF16) for i in range(2)]
            hstb = [Buf() for _ in range(2)]
            hsem = [Pg.dsem(f"p0hst{i}") for i in range(2)]
            ss = [sb(ph, f"p0_ss{i}", [128, 1], F32) for i in range(2)]
            ssb = [Buf() for _ in range(2)]
            rs = [sb(ph, f"p0_rs{i}", [128, 1], F32) for i in range(2)]
            rsb = [Buf() for _ in range(2)]
            gpk = sb(ph, "p0_gpk", [128, KC], F32)
            gpkb = Buf()
            Pg.dma("sp", gpk[:], nmix_pk, [], [gpkb], Pg.dsem("p0gpk"))

            jobs = [(x_all, i, 128, hT_all, i) for i in range(NT)] + \
                   [(x_own, i, cfg.rows(i), hT_own, i) for i in range(NOWN)]
            tcount = 0
            for n, (src, ti, rows, dst, di) in enumerate(jobs):
                s = n % 2
                Pg.dma("sp", xs[s][:rows, :], src[ti * 128: ti * 128 + rows, :], [], [xsb[s]], xsem[s])
                Pg.op("pool", lambda e, s=s: e.memset(ss[s][:], 0.0), [], [ssb[s]])
                Pg.op("act", lambda e, s=s, rows=rows: e.activation(out=junk[:rows, :], in_=xs[s][:rows, :], func=AF.Square,
                                                                    accum_out=ss[s][:rows, :]),
                      [xsb[s]], [junkb, ssb[s]])
                rstd_ops(ss[s][:rows, :], rs[s][:rows, :], ssb[s], rsb[s], D)
                Pg.op("dve", lambda e, s=s, rows=rows: e.tensor_scalar(out=hbf[s][:rows, :], in0=xs[s][:rows, :],
                                                                       scalar1=rs[s][:rows, :], scalar2=None, op0=ALU.mult),
                      [xsb[s], rsb[s]], [hbfb[s]])
                if rows < 128:
                    Pg.op("pool", lambda e, s=s: e.memset(hst[s][:], 0.0), [], [hstb[s]])
                for g0 in range(0, KC, 8):
                    g1 = min(KC, g0 + 8)
                    tb = tcount % 2
                    tcount += 1
                    items = [(TB_[tb][:, (kc - g0) * 128:(kc - g0) * 128 + rows], hbf[s][:rows, kc * 128:(kc + 1) * 128], rows)
                             for kc in range(g0, g1)]
                    Pg.transposes(items, ident, [hbfb[s], identb], [TBb[tb]])
                    Pg.op("dve", lambda e, s=s, tb=tb, g0=g0, g1=g1, rows=rows: e.tensor_tensor(
                        out=hst[s][:, g0:g1, :rows],
                        in0=TB_[tb][:, 0:(g1 - g0) * 128].rearrange("p (k t) -> p k t", t=128)[:, :, :rows],
                        in1=gpk[:, g0:g1].unsqueeze(2).to_broadcast([128, g1 - g0, rows]), op=ALU.mult),
                        [TBb[tb], gpkb], [hstb[s]])
                Pg.dma("sp", dst[di], hst[s][:].rearrange("p k t -> p (k t)"), [hstb[s]], [], hsem[s])
            Pg.flush()
            if stop == 0:
                Pg.flush()
                return nc

        with ExitStack() as ph:
            wq = sb(ph, "p1_wq", [128, KC, 256], BF16)
            wqb = Buf()
            wqs = Pg.dsem("p1wq")
            wkv = sb(ph, "p1_wkv", [128, KC, 512], BF16)
            wkvb = Buf()
            wkvs = Pg.dsem("p1wkv")
            NHT = 3
            hts = [sb(ph, f"p1_ht{i}", [128, KC, 128], BF16) for i in range(NHT)]
            htb = [Buf() for _ in range(NHT)]
            htsem = [Pg.dsem(f"p1ht{i}") for i in range(NHT)]
            cst = [sb(ph, f"p1_cs{i}", [128, 256], F32) for i in range(NHT)]
            cstb = [Buf() for _ in range(NHT)]
            cssem = [Pg.dsem(f"p1cs{i}") for i in range(NHT)]
            KT = sb(ph, "p1_KT", [128, 2, T], BF16)
            KTb = Buf(NT)
            Vh = sb(ph, "p1_V", [128, NT, 257], BF16)
            Vb = Buf(NT)
            QT = sb(ph, "p1_QT", [128, 2, NJ * 128], BF16)
            QTb = Buf(NJ)
            QTs = sb(ph, "p1_QTs", [128, 2, SROWS], BF16)
            QTsb = Buf()
            KTs = sb(ph, "p1_KTs", [128, 2, SROWS], BF16)
            KTsb = Buf()
            Vs = sb(ph, "p1_Vs", [SROWS, 257], BF16)
            Vsb = Buf()
            Kc = sb(ph, "p1_Kc", [128, PT, 256], BF16)
            Kcb = Buf()
            Kcs = Pg.dsem("p1kc")
            KTc = sb(ph, "p1_KTc", [128, 2, P], BF16)
            KTcb = Buf()
            Vc = sb(ph, "p1_Vc", [128, PT, 257], BF16)
            Vcb = Buf()
            Vcs = Pg.dsem("p1vc")
            NST = 3
            kvs = [sb(ph, f"p1_kvs{i}", [128, 768], F32) for i in range(NST)]
            kvsb = [Buf() for _ in range(NST)]
            kvsem = [Pg.dsem(f"p1kvs{i}") for i in range(NST)]
            ko = [sb(ph, f"p1_ko{i}", [128, 512], F32) for i in range(NST)]
            kob = [Buf() for _ in range(NST)]
            kosem = [Pg.dsem(f"p1ko{i}") for i in range(NST)]
            tmpa = [sb(ph, f"p1_tmpa{i}", [128, 256], F32) for i in range(2)]
            tmpab = [Buf() for _ in range(2)]
            tmpb = [sb(ph, f"p1_tmpb{i}", [128, 256], F32) for i in range(2)]
            tmpbb = [Buf() for _ in range(2)]
            knt = [sb(ph, f"p1_kn{i}", [128, 256], F32) for i in range(2)]
            kntb = [Buf() for _ in range(2)]
            kbf = [sb(ph, f"p1_kbf{i}", [128, 256], BF16) for i in range(4)]
            kbfb = [Buf() for _ in range(4)]
            ss4 = [sb(ph, f"p1_ss4{i}", [128, 4], F32) for i in range(2)]
            ss4b = [Buf() for _ in range(2)]
            rs4 = [sb(ph, f"p1_rs4{i}", [128, 4], F32) for i in range(2)]
            rs4b = [Buf() for _ in range(2)]
            PTt = [sb(ph, f"p1_PT{i}", [128, 512], BF16) for i in range(3)]
            PTb = [Buf() for _ in range(3)]
            PTn = sb(ph, "p1_PTn", [SROWS, 32], BF16)
            PTnb = Buf()
            maskt = sb(ph, "p1_mask", [128, 4, 128], BF16)
            maskb = Buf()
            qg = sb(ph, "p1_qg", [128, 128], F32)
            kg = sb(ph, "p1_kg", [128, 128], F32)
            gb = Buf()
            lam4 = sb(ph, "p1_lam4", [128, 4, 128], F32)
            lamt = sb(ph, "p1_lamt", [128, 2, 128], F32)
            lam2 = sb(ph, "p1_lam2", [128, 2], F32)
            neglam = sb(ph, "p1_neglam", [128, 1], F32)
            lamb = Buf()
            slg = sb(ph, "p1_slg", [128, 256], F32)
            slgb = Buf()
            fin_o0 = sb(ph, "p1_o0", [128, 256], F32)
            fin_o = sb(ph, "p1_o", [128, 256], F32)
            fin_sq = sb(ph, "p1_fsq", [128, 256], F32)
            finb = Buf()
            fin_r = sb(ph, "p1_fr", [128, 8], F32)
            finrb = Buf()
            oat = [sb(ph, f"p1_oat{i}", [128, 256], BF16) for i in range(2)]
            oatb = [Buf() for _ in range(2)]
            oatsem = [Pg.dsem(f"p1oat{i}") for i in range(2)]
            oats = sb(ph, "p1_oats", [SROWS, 256], BF16)
            oatsb = Buf()
            oatssem = Pg.dsem("p1oats")

            Pg.dma("pool", maskt[:].rearrange("p a b -> p (a b)"), masks_d, [], [maskb], Pg.dsem("p1mask"))
            csem = Pg.dsem("p1const")
            Pg.op("sp", lambda e: [e.dma_start(out=qg[:], in_=qn_bc), e.dma_start(out=kg[:], in_=kn_bc),
                                   e.dma_start(out=lam4[:].rearrange("p a b -> p (a b)"), in_=lam_d),
                                   e.dma_start(out=slg[:], in_=subln_bc)], [], [gb, lamb, slgb], sem=csem, ninc=4)
            Pg.op("dve", lambda e: e.tensor_tensor(out=lamt[:], in0=lam4[:].rearrange("p (a b) d -> p a b d", b=2)[:, :, 0, :], in1=lam4[:].rearrange("p (a b) d -> p a b d", b=2)[:, :, 1, :], op=ALU.mult),
                  [lamb], [lamb])
            Pg.op("dve", lambda e: e.tensor_reduce(out=lam2[:], in_=lamt[:], axis=AX.X, op=ALU.add), [lamb], [lamb])
            Pg.op("act", lambda e: e.activation(out=lam2[:], in_=lam2[:], func=AF.Exp), [lamb], [lamb])
            Pg.op("dve", lambda e: e.tensor_tensor(out=neglam[:], in0=lam2[:, 1:2], in1=lam2[:, 0:1], op=ALU.subtract),
                  [lamb], [lamb])
            Pg.op("dve", lambda e: e.tensor_scalar(out=neglam[:], in0=neglam[:], scalar1=-cfg.lambda_init, scalar2=None,
                                                   op0=ALU.add), [lamb], [lamb])
            Pg.op("dve", lambda e: e.tensor_scalar(out=slg[:], in0=slg[:], scalar1=1.0 - cfg.lambda_init, scalar2=None,
                                                   op0=ALU.mult), [slgb], [slgb])
            Pg.op("pool", lambda e: e.memset(Vh[:, :, 256:257], 1.0), [], [Vb])
            Pg.op("pool", lambda e: e.memset(Vc[:, :, 256:257], 1.0), [], [Vcb])
            Pg.op("pool", lambda e: e.memset(Vs[:, 256:257], 1.0), [], [Vsb])
            Pg.op("pool", lambda e: e.memset(oats[:], 0.0), [], [oatsb])

            w3 = w_in.rearrange("(k p) c -> p k c", p=128)

            state = {"st": 0, "tb": 0, "fb": 0, "ht": 0, "t2": 0, "kb": 0}
            pend = []
            DL = 2

            def load_ht(src, idx, cs_src, rows):
                i = state["ht"] % NHT
                state["ht"] += 1
                Pg.dma("sp", hts[i][:].rearrange("p k t -> p (k t)"), src[idx], [], [htb[i]], htsem[i])
                Pg.dma("sp", cst[i][:rows, :], cs_src, [], [cstb[i]], cssem[i])
                return i

            def norm_rope(src_ap, srcb, nsub, rows, gain, csi, dst_f32, dstb, dst_bf, dst_bfb):
                t = state["t2"] % 2
                state["t2"] += 1
                W = nsub * 128
                v3 = lambda ap: ap.rearrange("p (s d) -> p s d", d=128)
                Pg.op("dve", lambda e: e.tensor_tensor(out=tmpa[t][:rows, :W], in0=src_ap, in1=src_ap, op=ALU.mult),
                      [srcb], [tmpab[t]])
                Pg.op("dve", lambda e: e.tensor_reduce(out=ss4[t][:rows, :nsub], in_=v3(tmpa[t][:rows, :W]), axis=AX.X, op=ALU.add),
                      [tmpab[t]], [ss4b[t]])
                rstd_ops(ss4[t][:rows, :nsub], rs4[t][:rows, :nsub], ss4b[t], rs4b[t], 128)
                for s_ in range(nsub):
                    Pg.op("dve", lambda e, s_=s_: e.scalar_tensor_tensor(
                        out=knt[t][:rows, s_ * 128:(s_ + 1) * 128], in0=src_ap[:, s_ * 128:(s_ + 1) * 128],
                        scalar=rs4[t][:rows, s_:s_ + 1], in1=gain[:rows, :], op0=ALU.mult, op1=ALU.mult),
                        [srcb, rs4b[t], gb], [kntb[t].sub(0)] if False else [kntb[t]])
                kn3 = v3(knt[t][:rows, :W])
                cos_b = cst[csi][:rows, 0:128].unsqueeze(1).to_broadcast([rows, nsub, 128])
                Pg.op("dve", lambda e: e.tensor_tensor(out=v3(tmpa[t][:rows, :W]), in0=kn3, in1=cos_b, op=ALU.mult),
                      [kntb[t], cstb[csi]], [tmpab[t]])
                sinA = cst[csi][:rows, 128:192].unsqueeze(1).to_broadcast([rows, nsub, 64])
                sinB = cst[csi][:rows, 192:256].unsqueeze(1).to_broadcast([rows, nsub, 64])
                tb3 = v3(tmpb[t][:rows, :W])
                Pg.op("pool", lambda e: e.tensor_tensor(out=tb3[:, :, 0:64], in0=kn3[:, :, 64:128], in1=sinA, op=ALU.mult),
                      [kntb[t], cstb[csi]], [tmpbb[t]])
                Pg.op("pool", lambda e: e.tensor_tensor(out=tb3[:, :, 64:128], in0=kn3[:, :, 0:64], in1=sinB, op=ALU.mult),
                      [kntb[t], cstb[csi], tmpbb[t]], [tmpbb[t]])
                Pg.op("dve", lambda e: e.tensor_tensor(out=dst_f32, in0=tmpa[t][:rows, :W], in1=tmpb[t][:rows, :W], op=ALU.add),
                      [tmpab[t], tmpbb[t]], [dstb])
                Pg.op("pool", lambda e: e.tensor_copy(out=dst_bf, in_=dst_f32), [dstb], [dst_bfb])

            def finalize(acc0, acc1, accb0, accb1, p0, p1, dst_bf, dstb):
                sl = slice(p0, p1)
                Pg.op("dve", lambda e: e.reciprocal(out=fin_r[sl, 0:1], in_=acc0[sl, 256:257]), [accb0], [finrb])
                Pg.op("dve", lambda e: e.reciprocal(out=fin_r[sl, 1:2], in_=acc1[sl, 256:257]), [accb1, finrb], [finrb])
                Pg.op("dve", lambda e: e.tensor_scalar(out=fin_o0[sl, :], in0=acc0[sl, 0:256], scalar1=fin_r[sl, 0:1], scalar2=None,
                                                       op0=ALU.mult), [accb0, finrb], [finb])
                Pg.op("dve", lambda e: e.tensor_tensor(out=fin_r[sl, 2:3], in0=fin_r[sl, 1:2], in1=neglam[sl, :], op=ALU.mult),
                      [finrb, lamb], [finrb])
                Pg.op("dve", lambda e: e.scalar_tensor_tensor(out=fin_o[sl, :], in0=acc1[sl, 0:256], scalar=fin_r[sl, 2:3],
                                                              in1=fin_o0[sl, :], op0=ALU.mult, op1=ALU.add),
                      [accb1, finrb, finb], [finb])
                Pg.op("dve", lambda e: e.tensor_tensor(out=fin_sq[sl, :], in0=fin_o[sl, :], in1=fin_o[sl, :], op=ALU.mult),
                      [finb], [finb])
                Pg.op("dve", lambda e: e.tensor_reduce(out=fin_r[sl, 3:4], in_=fin_sq[sl, :], axis=AX.X, op=ALU.add),
                      [finb, finrb], [finrb])
                rstd_ops(fin_r[sl, 3:4], fin_r[sl, 4:5], finrb, finrb, 256)
                Pg.op("dve", lambda e: e.scalar_tensor_tensor(out=dst_bf, in0=fin_o[sl, :], scalar=fin_r[sl, 4:5],
                                                              in1=slg[sl, :], op0=ALU.mult, op1=ALU.mult),
                      [finb, finrb, slgb], [dstb])

            for h in range(H):
                Pg.op("pool", lambda e, h=h: [e.dma_start(out=wkv[:, :, 0:256], in_=w3[:, :, D + h * HW: D + (h + 1) * HW]),
                                              e.dma_start(out=wkv[:, :, 256:512], in_=w3[:, :, 2 * D + h * HW: 2 * D + (h + 1) * HW])],
                      [], [wkvb], sem=wkvs, ninc=2)
                Pg.op("pool", lambda e, h=h: [e.dma_start(out=wq[:], in_=w3[:, :, h * HW:(h + 1) * HW])], [], [wqb], sem=wqs)

                def sample_load(sbi, h=h):
                    Pg.op("pool", lambda e, sbi=sbi, h=h: [e.dma_start(out=Kc[:], in_=cache_k[sbi, :, h * HW:(h + 1) * HW].rearrange("(t p) c -> p t c", p=128))],
                          [], [Kcb], sem=Kcs)
                    Pg.op("pool", lambda e, sbi=sbi, h=h: [e.dma_start(out=Vc[:, :, 0:256], in_=cache_v[sbi, :, h * HW:(h + 1) * HW].rearrange("(t p) c -> p t c", p=128))],
                          [], [Vcb], sem=Vcs)

                def sample_attn(sbi, h=h):
                    r0 = sbi * 32
                    for t0 in range(0, PT, 4):
                        tb = state["tb"] % 2
                        state["tb"] += 1
                        t1 = min(PT, t0 + 4)
                        items = []
                        for t in range(t0, t1):
                            for s_ in range(2):
                                c0 = ((t - t0) * 2 + s_) * 128
                                items.append((TB_[tb][:, c0:c0 + 128], Kc[:, t, s_ * 128:(s_ + 1) * 128], 128))
                        Pg.transposes(items, ident, [Kcb, identb], [TBb[tb]])
                        n = t1 - t0
                        Pg.op("act", lambda e, tb=tb, t0=t0, n=n: e.activation(
                            out=KTc[:, :, t0 * 128:(t0 + n) * 128].rearrange("p s (t k) -> p t s k", k=128),
                            in_=TB_[tb][:, 0:n * 256].rearrange("p (t s k) -> p t s k", s=2, k=128), func=AF.Copy),
                            [TBb[tb]], [KTcb])
                    for s_ in range(2):
                        Pg.mm_group([(FB_[s_][:, t * 16:(t + 1) * 16], KTc[:, s_, t * 128:(t + 1) * 128], QTs[:, s_, r0:r0 + 16], True, True)
                                     for t in range(PT)], [KTcb, QTsb], [FBb[s_]])
                    Pg.mm_group([(FB_[4][r0:r0 + 16, s_ * 16:(s_ + 1) * 16], KTs[:, s_, r0:r0 + 16], QTs[:, s_, r0:r0 + 16], True, True)
                                 for s_ in range(2)], [KTsb, QTsb], [FBb[4]])
                    for s_ in range(2):
                        Pg.op("act", lambda e, s_=s_: e.activation(out=PTt[s_][:, 0:PT * 16], in_=FB_[s_][:, 0:PT * 16], func=AF.Exp, scale=scale),
                              [FBb[s_]], [PTb[s_]])
                    Pg.op("act", lambda e, r0=r0: e.activation(out=PTn[r0:r0 + 16, :], in_=FB_[4][r0:r0 + 16, 0:32], func=AF.Exp, scale=scale),
                          [FBb[4]], [PTnb])
                    for s_ in range(2):
                        mms = [(FB_[2 + s_][r0:r0 + 16, 0:257], PTt[s_][:, t * 16:(t + 1) * 16], Vc[:, t, :], t == 0, False) for t in range(PT)]
                        mms.append((FB_[2 + s_][r0:r0 + 16, 0:257], PTn[r0:r0 + 16, s_ * 16:(s_ + 1) * 16], Vs[r0:r0 + 16, :], False, True))
                        Pg.mm_group(mms, [PTb[s_], PTnb, Vcb, Vsb], [FBb[2 + s_]])
                    finalize(FB_[2], FB_[3], FBb[2], FBb[3], r0, r0 + 16, oats[r0:r0 + 16, :], oatsb)

                sample_load(0)
                nxt = load_ht(hT_all, 0, cs_all[0:128, :], 128)
                for i in range(NT):
                    hi = nxt
                    if i + 1 < NT:
                        nxt = load_ht(hT_all, i + 1, cs_all[(i + 1) * 128:(i + 2) * 128, :], 128)
                    fb = state["fb"] % 4
                    state["fb"] += 1
                    st = state["st"] % NST
                    state["st"] += 1
                    Pg.mm_group([(FB_[fb][:, 0:512], hts[hi][:, kc, :], wkv[:, kc, :], kc == 0, kc == KC - 1) for kc in range(KC)],
                                [htb[hi], wkvb], [FBb[fb]])
                    while len(pend) >= DL:
                        pend.pop(0)()
                    Pg.op("act", lambda e, fb=fb, st=st: e.activation(out=kvs[st][:, 0:512], in_=FB_[fb][:, 0:512], func=AF.Copy),
                          [FBb[fb]], [kvsb[st]])
                    t2 = state["kb"] % 4
                    state["kb"] += 1
                    norm_rope(kvs[st][:, 0:256], kvsb[st], 2, 128, kg, hi, ko[st][:, 0:256], kob[st], kbf[t2][:, 0:256], kbfb[t2])

                    def tr_k(t2=t2, i=i):
                        tb = state["tb"] % 2
                        state["tb"] += 1
                        Pg.transposes([(TB_[tb][:, s_ * 128:(s_ + 1) * 128], kbf[t2][:, s_ * 128:(s_ + 1) * 128], 128) for s_ in range(2)],
                                      ident, [kbfb[t2], identb], [TBb[tb]])
                        Pg.op("act", lambda e, tb=tb, i=i: e.activation(out=KT[:, :, i * 128:(i + 1) * 128],
                                                                        in_=TB_[tb][:, 0:256].rearrange("p (s t) -> p s t", t=128), func=AF.Copy),
                              [TBb[tb]], [KTb.sub(i)])
                    pend.append(tr_k)
                    Pg.op("pool", lambda e, st=st, i=i: e.tensor_copy(out=Vh[:, i, 0:256], in_=kvs[st][:, 256:512]),
                          [kvsb[st]], [Vb.sub(i)])
                    Pg.dma("pool", k_all[i * 128:(i + 1) * 128, h * HW:(h + 1) * HW], ko[st][:, 0:256], [kob[st]], [], kosem[st])
                    Pg.dma("pool", v_all[i * 128:(i + 1) * 128, h * HW:(h + 1) * HW], kvs[st][:, 256:512], [kvsb[st]], [], kvsem[st])

                nxt = load_ht(hT_own, 0, cs_own[0:cfg.rows(0), :], cfg.rows(0))
                for j in range(NOWN):
                    rows = cfg.rows(j)
                    hi = nxt
                    if j + 1 < NOWN:
                        r2 = cfg.rows(j + 1)
                        nxt = load_ht(hT_own, j + 1, cs_own[(j + 1) * 128:(j + 1) * 128 + r2, :], r2)
                    fb = state["fb"] % 4
                    state["fb"] += 1
                    st = state["st"] % NST
                    state["st"] += 1
                    Pg.mm_group([(FB_[fb][:rows, 0:256], hts[hi][:, kc, :rows], wq[:, kc, :], kc == 0, kc == KC - 1) for kc in range(KC)],
                                [htb[hi], wqb], [FBb[fb]])
                    while len(pend) >= DL:
                        pend.pop(0)()
                    Pg.op("act", lambda e, fb=fb, st=st, rows=rows: e.activation(out=kvs[st][:rows, 512:768], in_=FB_[fb][:rows, 0:256], func=AF.Copy),
                          [FBb[fb]], [kvsb[st]])
                    if j < NJ:
                        t2 = state["kb"] % 4
                        state["kb"] += 1
                        norm_rope(kvs[st][:, 512:768], kvsb[st], 2, 128, qg, hi, ko[st][:, 256:512], kob[st], kbf[t2][:, 0:256], kbfb[t2])

                        def tr_q(t2=t2, j=j):
                            tb = state["tb"] % 2
                            state["tb"] += 1
                            Pg.transposes([(TB_[tb][:, s_ * 128:(s_ + 1) * 128], kbf[t2][:, s_ * 128:(s_ + 1) * 128], 128) for s_ in range(2)],
                                          ident, [kbfb[t2], identb], [TBb[tb]])
                            Pg.op("act", lambda e, tb=tb, j=j: e.activation(out=QT[:, :, j * 128:(j + 1) * 128],
                                                                            in_=TB_[tb][:, 0:256].rearrange("p (s t) -> p s t", t=128), func=AF.Copy),
                                  [TBb[tb]], [QTb.sub(j)])
                        pend.append(tr_q)
                    else:
                        while pend:
                            pend.pop(0)()
                        R = SROWS
                        fb2 = state["fb"] % 4
                        state["fb"] += 1
                        Pg.mm_group([(FB_[fb2][:R, 0:512], hts[hi][:, kc, :R], wkv[:, kc, :], kc == 0, kc == KC - 1) for kc in range(KC)],
                                    [htb[hi], wkvb], [FBb[fb2]])
                        Pg.op("act", lambda e, fb2=fb2, st=st: e.activation(out=kvs[st][:R, 0:512], in_=FB_[fb2][:R, 0:512], func=AF.Copy),
                              [FBb[fb2]], [kvsb[st]])
                        t2 = state["kb"] % 4
                        state["kb"] += 1
                        norm_rope(kvs[st][:R, 512:768], kvsb[st], 2, R, qg, hi, ko[st][:R, 256:512], kob[st], kbf[t2][:R, 0:256], kbfb[t2])
                        tb = state["tb"] % 2
                        state["tb"] += 1
                        Pg.transposes([(TB_[tb][:, s_ * R:(s_ + 1) * R], kbf[t2][:R, s_ * 128:(s_ + 1) * 128], R) for s_ in range(2)],
                                      ident, [kbfb[t2], identb], [TBb[tb]])
                        Pg.op("act", lambda e, tb=tb: e.activation(out=QTs[:], in_=TB_[tb][:, 0:2 * R].rearrange("p (s t) -> p s t", t=R), func=AF.Copy),
                              [TBb[tb]], [QTsb])
                        t2 = state["kb"] % 4
                        state["kb"] += 1
                        norm_rope(kvs[st][:R, 0:256], kvsb[st], 2, R, kg, hi, ko[st][:R, 0:256], kob[st], kbf[t2][:R, 0:256], kbfb[t2])
                        tb = state["tb"] % 2
                        state["tb"] += 1
                        Pg.transposes([(TB_[tb][:, s_ * R:(s_ + 1) * R], kbf[t2][:R, s_ * 128:(s_ + 1) * 128], R) for s_ in range(2)],
                                      ident, [kbfb[t2], identb], [TBb[tb]])
                        Pg.op("act", lambda e, tb=tb: e.activation(out=KTs[:], in_=TB_[tb][:, 0:2 * R].rearrange("p (s t) -> p s t", t=R), func=AF.Copy),
                              [TBb[tb]], [KTsb])
                        Pg.op("pool", lambda e, st=st: e.tensor_copy(out=Vs[:, 0:256], in_=kvs[st][:R, 256:512]), [kvsb[st]], [Vsb])
                        Pg.dma("pool", k_s[:, h * HW:(h + 1) * HW], ko[st][:R, 0:256], [kob[st]], [], kosem[st])
                        Pg.dma("pool", v_s[:, h * HW:(h + 1) * HW], kvs[st][:R, 256:512], [kvsb[st]], [], kvsem[st])

                sample_attn(0)
                sample_load(1)
                for j in range(NJ):
                    nkt = 4 * j + 4
                    npair = nkt // 2
                    a0, a1 = 2 + 2 * (j % 2), 3 + 2 * (j % 2)

                    def scores(p, j=j):
                        sbk = p % 2
                        mms = []
                        for kl in range(2):
                            kt = 2 * p + kl
                            for s_ in range(2):
                                c0 = (kl * 2 + s_) * 128
                                mms.append((FB_[sbk][:, c0:c0 + 128], KT[:, s_, kt * 128:(kt + 1) * 128], QT[:, s_, j * 128:(j + 1) * 128], True, True))
                        Pg.mm_group(mms, [KTb.sub(2 * p, 2 * p + 2), QTb.sub(j)], [FBb[sbk]])

                    scores(0)
                    if npair > 1:
                        scores(1)
                    for p in range(npair):
                        sbk = p % 2
                        pt = (j * 64 + p) % 3
                        Pg.op("act", lambda e, sbk=sbk, pt=pt: e.activation(out=PTt[pt][:], in_=FB_[sbk][:, 0:512], func=AF.Exp, scale=scale),
                              [FBb[sbk]], [PTb[pt]])
                        if p >= npair - 2:
                            m0 = 2 * (p - (npair - 2))
                            Pg.op("dve", lambda e, pt=pt, m0=m0: e.tensor_tensor(
                                out=PTt[pt][:].rearrange("p (k s q) -> p k s q", k=2, s=2),
                                in0=PTt[pt][:].rearrange("p (k s q) -> p k s q", k=2, s=2),
                                in1=maskt[:, m0:m0 + 2, :].unsqueeze(2).to_broadcast([128, 2, 2, 128]), op=ALU.mult),
                                [PTb[pt], maskb], [PTb[pt]])
                        mms = []
                        for kl in range(2):
                            kt = 2 * p + kl
                            for s_, ab in ((0, a0), (1, a1)):
                                c0 = (kl * 2 + s_) * 128
                                mms.append((FB_[ab][:, 0:257], PTt[pt][:, c0:c0 + 128], Vh[:, kt, :], kt == 0, kt == nkt - 1))
                        Pg.mm_group(mms, [PTb[pt], Vb.sub(2 * p, 2 * p + 2)], [FBb[a0], FBb[a1]])
                        if p + 2 < npair:
                            scores(p + 2)
                    o = j % 2
                    finalize(FB_[a0], FB_[a1], FBb[a0], FBb[a1], 0, 128, oat[o][:, :], oatb[o])
                    Pg.dma("pool", oatt_s[j * 128:(j + 1) * 128, h * HW:(h + 1) * HW], oat[o][:], [oatb[o]], [], oatsem[o])

                sample_attn(1)
                Pg.dma("pool", oatt_s[NJ * 128:NJ * 128 + SROWS, h * HW:(h + 1) * HW], oats[:], [oatsb], [], oatssem)
            Pg.flush()
            if stop == 1:
                Pg.flush()
                return nc

        GT = 3
        groups = [list(range(i, min(NOWN, i + GT))) for i in range(0, NOWN, GT)]
        with ExitStack() as ph:
            R1 = sb(ph, "p2_R1", [128, GT, KC, 128], BF16)
            R1b = Buf(GT)
            R1sem = [Pg.dsem(f"p2R1_{i}") for i in range(GT)]
            Mb = Buf(2 * GT)
            NWB = 5
            WBE = max(KC * CB, 2 * D)
            WB = [sb(ph, f"p2_WB{i}", [128, WBE], BF16) for i in range(NWB)]
            WBb = [Buf() for _ in range(NWB)]
            WBsem = [Pg.dsem(f"p2WB{i}") for i in range(NWB)]
            ssv = sb(ph, "p2_ssv", [128, GT, NCB], F32)
            ssvb = Buf()
            rsv = sb(ph, "p2_rsv", [128, GT], F32)
            rsvb = Buf()
            sgin = [sb(ph, f"p2_sgin{i}", [SROWS, CB], F32) for i in range(2)]
            sginb = [Buf() for _ in range(2)]
            sginsem = [Pg.dsem(f"p2sgin{i}") for i in range(2)]
            vs32b = Buf()
            wsT = sb(ph, "p2_wsT", [128, G, 128], F32)
            wsS = sb(ph, "p2_wsS", [SROWS, G, SROWS], F32)
            trl = sb(ph, "p2_tril", [128, 128], F32)
            trls = sb(ph, "p2_trils", [SROWS, SROWS], F32)
            wsb = Buf()
            Wm = sb(ph, "p2_Wm", [128, GT, G, 128], BF16)
            Wmb = Buf(GT)
            bst = sb(ph, "p2_bst", [128, G], F32)
            bss = sb(ph, "p2_bss", [SROWS, G], F32)
            bsb = Buf()
            gpk = sb(ph, "p2_gpk", [128, KC], F32)
            gpkb = Buf()
            NBC = 2
            bcA = [sb(ph, f"p2_bcA{i}", [128, CB], F32) for i in range(NBC)]
            bcB = [sb(ph, f"p2_bcB{i}", [128, CB], F32) for i in range(NBC)]
            bcS = [sb(ph, f"p2_bcS{i}", [128, CB], F32) for i in range(NBC)]
            bcb = [Buf() for _ in range(NBC)]
            bcsem = [Pg.dsem(f"p2bc{i}") for i in range(NBC)]
            oa = [sb(ph, f"p2_oa{i}", [128, CB], BF16) for i in range(2)]
            oab = [Buf() for _ in range(2)]
            oasem = [Pg.dsem(f"p2oa{i}") for i in range(2)]
            e1 = [sb(ph, f"p2_e1{i}", [128, CB], F32) for i in range(2)]
            e2 = [sb(ph, f"p2_e2{i}", [128, CB], F32) for i in range(2)]
            e3 = [sb(ph, f"p2_e3{i}", [128, CB], F32) for i in range(2)]
            e4 = [sb(ph, f"p2_e4{i}", [128, CB], F32) for i in range(2)]
            eb = [[Buf() for _ in range(4)] for _ in range(2)]
            sgst = [sb(ph, f"p2_sgst{i}", [SROWS, CB], F32) for i in range(2)]
            sgstb = [Buf() for _ in range(2)]
            sgsem = [Pg.dsem(f"p2sg{i}") for i in range(2)]
            junk = sb(ph, "p2_junk", [128, 512], BF16)
            junkb = Buf()
            hfs = [sb(ph, f"p2_hfs{i}", [128, D], BF16) for i in range(1)]
            hfsb = [Buf() for _ in range(1)]
            FBLK = 512 if F >= 512 else F
            NFC = FBLK // 128
            actT = [sb(ph, f"p2_actT{i}", [128, NFC, GT * 128], BF16) for i in range(2)]
            actTb = [Buf() for _ in range(2)]
            rl = [sb(ph, f"p2_rl{i}", [128, 512], F32) for i in range(2)]
            rlb = [Buf() for _ in range(2)]
            ysem = [Pg.dsem(f"p2y{i}") for i in range(GT)]

            csem = Pg.dsem("p2const")
            Pg.op("sp", lambda e: [e.dma_start(out=wsT[:].rearrange("p g t -> p (g t)"), in_=wsT_d),
                                   e.dma_start(out=wsS[:].rearrange("p g t -> p (g t)"), in_=wsS_d),
                                   e.dma_start(out=trl[:], in_=tril_d), e.dma_start(out=trls[:], in_=trils_d),
                                   e.dma_start(out=bst[:], in_=bs_t), e.dma_start(out=bss[:], in_=bs_s),
                                   e.dma_start(out=gpk[:], in_=nffn_pk)],
                  [], [wsb, bsb, gpkb], sem=csem, ninc=7)
            Pg.op("dve", lambda e: e.tensor_tensor(out=wsT[:], in0=wsT[:], in1=trl[:].unsqueeze(1).to_broadcast([128, G, 128]), op=ALU.mult),
                  [wsb], [wsb])
            Pg.op("dve", lambda e: e.tensor_tensor(out=wsS[:], in0=wsS[:], in1=trls[:].unsqueeze(1).to_broadcast([SROWS, G, SROWS]), op=ALU.mult),
                  [wsb], [wsb])

            w3 = w_in.rearrange("(k p) c -> p k c", p=128)
            wo3 = w_o.rearrange("(k p) c -> p k c", p=128)
            wu3 = w_up.rearrange("(k p) c -> p k c", p=128)
            wd3 = w_down.rearrange("(k p) c -> p k c", p=128)
            st2 = {"wb": 0, "fb": 0, "tb": 0, "bc": 0, "e": 0, "oa": 0, "sg": 0}

            def load_w(src3, c0, ncols):
                i = st2["wb"] % NWB
                st2["wb"] += 1
                assert not wpend[i], "weight slot overwritten before its consumers were generated"
                wpend[i] = True
                Pg.op("pool", lambda e, i=i: [e.dma_start(out=WB[i][:, 0:KC * ncols].rearrange("p (k c) -> p k c", c=ncols),
                                                          in_=src3[:, :, c0:c0 + ncols])], [], [WBb[i]], sem=WBsem[i])
                return i

            wpend = [False] * NWB

            def rel(*slots):
                for i in slots:
                    wpend[i] = False

            def wv(i, ncols):
                return WB[i][:, 0:KC * ncols].rearrange("p (k c) -> p k c", c=ncols)

            for gidx, tiles in enumerate(groups):
                ng = len(tiles)
                gs1 = ExitStack()
                Mbf = sb(gs1, f"p2_Mbf{gidx}", [128, GT, 2 * D], BF16)

                def vsr(gi, Mbf=Mbf):
                    return Mbf[:, gi, 0:D]

                def mrg(gi, Mbf=Mbf):
                    return Mbf[:, gi, D:2 * D]
                rws = [cfg.rows(t) for t in tiles]
                has_s = tiles[-1] == NOWN - 1
                for gi, t in enumerate(tiles):
                    Pg.dma("sp", R1[:, gi].rearrange("p k t -> p (k t)"), hT_own[t], [], [R1b.sub(gi)], R1sem[gi])
                Pg.op("pool", lambda e: e.memset(ssv[:], 0.0), [], [ssvb])
                import os as _os
                _skip = _os.environ.get("MK_SKIP", "")
                nxt = load_w(w3, 4 * D, CB)
                for cb in range(NCB if "A" not in _skip else 0):
                    wi = nxt
                    if cb + 1 < NCB:
                        nxt = load_w(w3, 4 * D + (cb + 1) * CB, CB)
                    for gi, t in enumerate(tiles):
                        rows = rws[gi]
                        fb = st2["fb"] % 4
                        st2["fb"] += 1
                        Pg.mm_group([(FB_[fb][:rows, 0:CB], R1[:, gi, kc, :rows], wv(wi, CB)[:, kc, :], kc == 0, kc == KC - 1) for kc in range(KC)],
                                    [R1b.sub(gi), WBb[wi]], [FBb[fb]])
                        if "D" not in _skip:
                          Pg.op("act", lambda e, fb=fb, gi=gi, cb=cb, rows=rows: e.activation(
                            out=junk[:rows, 0:CB], in_=FB_[fb][:rows, 0:CB], func=AF.Square, accum_out=ssv[:rows, gi, cb:cb + 1]),
                            [FBb[fb]], [junkb, ssvb])
                        if "B" not in _skip:
                          Pg.op("act", lambda e, fb=fb, gi=gi, cb=cb, rows=rows, vsr=vsr: e.activation(out=vsr(gi)[:rows, cb * CB:(cb + 1) * CB], in_=FB_[fb][:rows, 0:CB], func=AF.Copy),
                              [FBb[fb]], [Mb.sub(2 * gi)])
                        if has_s and gi == ng - 1 and "C" not in _skip:
                            si = st2["sg"] % 2
                            st2["sg"] += 1
                            Pg.op("act", lambda e, fb=fb, si=si, rows=rows: e.activation(out=sgst[si][:, :], in_=FB_[fb][:rows, 0:CB], func=AF.Copy),
                                  [FBb[fb]], [sgstb[si]])
                            Pg.dma("sp", vs32_s[:, cb * CB:(cb + 1) * CB], sgst[si][:, :], [sgstb[si]], [vs32b], sgsem[si])
                    rel(wi)
                if stop == 2:
                    Pg.flush(); Pg.flush()
                    gs1.close()
                    return nc
                Pg.op("dve", lambda e: e.tensor_reduce(out=rsv[:], in_=ssv[:], axis=AX.X, op=ALU.add), [ssvb], [rsvb])
                rstd_ops(rsv[:], rsv[:], rsvb, rsvb, D)
                for gi, t in enumerate(tiles):
                    rows = rws[gi]
                    if rows == 128:
                        Pg.op("dve", lambda e, gi=gi: e.tensor_scalar(out=Wm[:, gi], in0=wsT[:], scalar1=rsv[:, gi:gi + 1], scalar2=None, op0=ALU.mult),
                              [wsb, rsvb], [Wmb.sub(gi)])
                    else:
                        Pg.op("dve", lambda e, gi=gi, rows=rows: e.tensor_scalar(out=Wm[:rows, gi, :, :rows], in0=wsS[:], scalar1=rsv[:rows, gi:gi + 1],
                                                                                 scalar2=None, op0=ALU.mult),
                              [wsb, rsvb], [Wmb.sub(gi)])
                if stop == 3:
                    Pg.flush(); Pg.flush()
                    gs1.close()
                    return nc
                def load_cb1(cb):
                    wu_ = load_w(w3, 3 * D + cb * CB, CB)
                    wa_ = load_w(w3, 5 * D + cb * CB, CB)
                    return (wu_, wa_)

                def load_cb2(cb):
                    wb_ = load_w(w3, 6 * D + cb * CB, CB)
                    bi = st2["bc"] % NBC
                    st2["bc"] += 1
                    Pg.op("sp", lambda e, bi=bi, cb=cb: [e.dma_start(out=bcA[bi][:], in_=bgate_bc[:, cb * CB:(cb + 1) * CB]),
                                                        e.dma_start(out=bcB[bi][:], in_=bgate_bc[:, D + cb * CB: D + (cb + 1) * CB]),
                                                        e.dma_start(out=bcS[bi][:], in_=sgu_bc[:, cb * CB:(cb + 1) * CB])],
                          [], [bcb[bi]], sem=bcsem[bi], ninc=3)
                    return (wb_, bi)

                nxt1 = load_cb1(0)
                nxt2 = load_cb2(0)
                for cb in range(NCB):
                    wu_, wa_ = nxt1
                    wb_, bi = nxt2
                    if cb + 1 < NCB:
                        nxt1 = load_cb1(cb + 1)
                    segs = []
                    c = cb * CB
                    while c < (cb + 1) * CB:
                        g = c // GD
                        ce = min((g + 1) * GD, (cb + 1) * CB)
                        segs.append((g, c - cb * CB, ce - cb * CB))
                        c = ce
                    for gi, t in enumerate(tiles):
                        rows = rws[gi]
                        is_s = rows < 128
                        fa = (st2["fb"] % 2) * 2
                        st2["fb"] += 1
                        f1, f2 = FB_[fa], FB_[fa + 1]
                        f1b, f2b = FBb[fa], FBb[fa + 1]
                        ei = st2["e"] % 2
                        st2["e"] += 1
                        oi = st2["oa"] % 2
                        st2["oa"] += 1
                        Pg.dma("sp", oa[oi][:rows, :], oatt_s[t * 128:t * 128 + rows, cb * CB:(cb + 1) * CB], [], [oab[oi]], oasem[oi])
                        mms = [(f1[:rows, 0:CB], R1[:, gi, kc, :rows], wv(wu_, CB)[:, kc, :], kc == 0, kc == KC - 1) for kc in range(KC)]
                        mms += [(f1[:rows, CB:2 * CB], R1[:, gi, kc, :rows], wv(wa_, CB)[:, kc, :], kc == 0, kc == KC - 1) for kc in range(KC)]
                        Pg.mm_group(mms, [R1b.sub(gi), WBb[wu_], WBb[wa_]], [f1b])
                        mms = [(f2[:rows, 0:CB], R1[:, gi, kc, :rows], wv(wb_, CB)[:, kc, :], kc == 0, kc == KC - 1) for kc in range(KC)]
                        for (g, s0, s1) in segs:
                            mms.append((f2[:rows, CB + s0:CB + s1], Wm[:rows, gi, g, :rows], vsr(gi)[:rows, cb * CB + s0:cb * CB + s1], True, True))
                        Pg.mm_group(mms, [R1b.sub(gi), WBb[wb_], Wmb.sub(gi), Mb.sub(2 * gi)], [f2b])
                        Pg.op("dve", lambda e, ei=ei, bi=bi, rows=rows, f1=f1: e.tensor_tensor(out=e1[ei][:rows, :], in0=f1[:rows, CB:2 * CB], in1=bcA[bi][:rows, :], op=ALU.add),
                              [f1b, bcb[bi]], [eb[ei][0]])
                        Pg.op("act", lambda e, ei=ei, rows=rows: e.activation(out=e1[ei][:rows, :], in_=e1[ei][:rows, :], func=AF.Sigmoid), [eb[ei][0]], [eb[ei][0]])
                        Pg.op("dve", lambda e, ei=ei, bi=bi, rows=rows, f2=f2: e.tensor_tensor(out=e2[ei][:rows, :], in0=f2[:rows, 0:CB], in1=bcB[bi][:rows, :], op=ALU.add),
                              [f2b, bcb[bi]], [eb[ei][1]])
                        Pg.op("act", lambda e, ei=ei, rows=rows: e.activation(out=e2[ei][:rows, :], in_=e2[ei][:rows, :], func=AF.Sigmoid), [eb[ei][1]], [eb[ei][1]])
                        Pg.op("dve", lambda e, ei=ei, bi=bi, rows=rows, f2=f2: e.tensor_tensor(out=e3[ei][:rows, :], in0=f2[:rows, CB:2 * CB], in1=bcS[bi][:rows, :], op=ALU.mult),
                              [f2b, bcb[bi]], [eb[ei][2]])
                        bsrc = bss if is_s else bst
                        for (g, s0, s1) in segs:
                            Pg.op("dve", lambda e, ei=ei, rows=rows, f1=f1, g=g, s0=s0, s1=s1, bsrc=bsrc: e.scalar_tensor_tensor(
                                out=e3[ei][:rows, s0:s1], in0=e3[ei][:rows, s0:s1], scalar=bsrc[:rows, g:g + 1], in1=f1[:rows, s0:s1],
                                op0=ALU.add, op1=ALU.mult), [f1b, bsb, eb[ei][2]], [eb[ei][2]])
                        Pg.op("pool", lambda e, ei=ei, oi=oi, rows=rows: e.tensor_tensor(out=e1[ei][:rows, :], in0=e1[ei][:rows, :], in1=oa[oi][:rows, :], op=ALU.mult),
                              [eb[ei][0], oab[oi]], [eb[ei][0]])
                        Pg.op("dve", lambda e, ei=ei, rows=rows: e.tensor_tensor(out=e2[ei][:rows, :], in0=e2[ei][:rows, :], in1=e3[ei][:rows, :], op=ALU.mult),
                              [eb[ei][1], eb[ei][2]], [eb[ei][1]])
                        Pg.op("dve", lambda e, ei=ei, gi=gi, cb=cb, rows=rows: e.tensor_tensor(out=mrg(gi)[:rows, cb * CB:(cb + 1) * CB], in0=e1[ei][:rows, :], in1=e2[ei][:rows, :], op=ALU.add),
                              [eb[ei][0], eb[ei][1]], [Mb.sub(2 * gi + 1)])
                        if is_s:
                            si = st2["sg"] % 2
                            st2["sg"] += 1
                            Pg.dma("sp", sgin[si][:, :], vs32_s[:, cb * CB:(cb + 1) * CB], [vs32b], [sginb[si]], sginsem[si])
                            Pg.op("dve", lambda e, si=si, gi=gi, bi=bi, cb=cb, rows=rows: e.scalar_tensor_tensor(
                                out=sgst[si][:, :], in0=sgin[si][:, :], scalar=rsv[:rows, gi:gi + 1], in1=bcS[bi][:rows, :],
                                op0=ALU.mult, op1=ALU.mult), [sginb[si], rsvb, bcb[bi]], [sgstb[si]])
                            Pg.dma("sp", sgu_s[:, cb * CB:(cb + 1) * CB], sgst[si][:, :], [sgstb[si]], [], sgsem[si])
                    rel(wu_, wa_, wb_)
                    if cb + 1 < NCB:
                        nxt2 = load_cb2(cb + 1)
                if stop == 4:
                    Pg.flush(); Pg.flush()
                    gs1.close()
                    return nc
                for gi, t in enumerate(tiles):
                    rows = rws[gi]
                    for g0 in range(0, KC, 8):
                        g1 = min(KC, g0 + 8)
                        tb = st2["tb"] % 2
                        st2["tb"] += 1
                        items = [(TB_[tb][:, (kc - g0) * 128:(kc - g0) * 128 + rows], mrg(gi)[:rows, kc * 128:(kc + 1) * 128], rows) for kc in range(g0, g1)]
                        Pg.transposes(items, ident, [Mb.sub(2 * gi + 1), identb], [TBb[tb]])
                        Pg.op("act", lambda e, tb=tb, gi=gi, g0=g0, g1=g1, rows=rows: e.activation(
                            out=R1[:, gi, g0:g1, :rows], in_=TB_[tb][:, 0:(g1 - g0) * 128].rearrange("p (k t) -> p k t", t=128)[:, :, :rows], func=AF.Copy),
                            [TBb[tb]], [R1b.sub(gi)])
                Pg.flush()
                gs1.close()
                gs2 = ExitStack()
                M = sb(gs2, f"p2_Y{gidx}", [128, GT, D], F32)
                for gi, t in enumerate(tiles):
                    rows = rws[gi]
                    Pg.dma("sp", M[:rows, gi, :], x_own[t * 128:t * 128 + rows, :], [], [Mb.sub(2 * gi, 2 * gi + 2)], ysem[gi])
                Pg.op("pool", lambda e: e.memset(ssv[:], 0.0), [], [ssvb])
                nxt = load_w(wo3, 0, CB)
                for cb in range(NCB):
                    wi = nxt
                    if cb + 1 < NCB:
                        nxt = load_w(wo3, (cb + 1) * CB, CB)
                    for gi, t in enumerate(tiles):
                        rows = rws[gi]
                        fb = st2["fb"] % 4
                        st2["fb"] += 1
                        Pg.mm_group([(FB_[fb][:rows, 0:CB], R1[:, gi, kc, :rows], wv(wi, CB)[:, kc, :], kc == 0, kc == KC - 1) for kc in range(KC)],
                                    [R1b.sub(gi), WBb[wi]], [FBb[fb]])
                        Pg.op("dve", lambda e, fb=fb, gi=gi, cb=cb, rows=rows: e.tensor_tensor(
                            out=M[:rows, gi, cb * CB:(cb + 1) * CB], in0=FB_[fb][:rows, 0:CB], in1=M[:rows, gi, cb * CB:(cb + 1) * CB], op=ALU.add),
                            [FBb[fb]], [Mb.sub(2 * gi, 2 * gi + 2)])
                        Pg.op("act", lambda e, gi=gi, cb=cb, rows=rows: e.activation(
                            out=junk[:rows, 0:CB], in_=M[:rows, gi, cb * CB:(cb + 1) * CB], func=AF.Square, accum_out=ssv[:rows, gi, cb:cb + 1]),
                            [Mb.sub(2 * gi, 2 * gi + 2)], [junkb, ssvb])
                    rel(wi)
                Pg.op("dve", lambda e: e.tensor_reduce(out=rsv[:], in_=ssv[:], axis=AX.X, op=ALU.add), [ssvb], [rsvb])
                rstd_ops(rsv[:], rsv[:], rsvb, rsvb, D)
                ntok = ng * 128
                R1f = R1[:].rearrange("p g k t -> p (g k t)")[:, 0:KC * ntok].rearrange("p (k n) -> p k n", n=ntok)
                for gi, t in enumerate(tiles):
                    rows = rws[gi]
                    hs = 0
                    Pg.op("dve", lambda e, gi=gi, hs=hs, rows=rows: e.tensor_scalar(out=hfs[hs][:rows, :], in0=M[:rows, gi, :], scalar1=rsv[:rows, gi:gi + 1],
                                                                                   scalar2=None, op0=ALU.mult),
                          [Mb.sub(2 * gi, 2 * gi + 2), rsvb], [hfsb[hs]])
                    for g0 in range(0, KC, 8):
                        g1 = min(KC, g0 + 8)
                        tb = st2["tb"] % 2
                        st2["tb"] += 1
                        items = [(TB_[tb][:, (kc - g0) * 128:(kc - g0) * 128 + rows], hfs[hs][:rows, kc * 128:(kc + 1) * 128], rows) for kc in range(g0, g1)]
                        Pg.transposes(items, ident, [hfsb[hs], identb], [TBb[tb]])
                        Pg.op("dve", lambda e, tb=tb, gi=gi, g0=g0, g1=g1, rows=rows: e.tensor_tensor(
                            out=R1f[:, g0:g1, gi * 128:gi * 128 + rows],
                            in0=TB_[tb][:, 0:(g1 - g0) * 128].rearrange("p (k t) -> p k t", t=128)[:, :, :rows],
                            in1=gpk[:, g0:g1].unsqueeze(2).to_broadcast([128, g1 - g0, rows]), op=ALU.mult),
                            [TBb[tb], gpkb], [R1b])
                if stop == 5:
                    Pg.flush(); Pg.flush()
                    gs2.close()
                    return nc
                for gi, t in enumerate(tiles):
                    rows = rws[gi]
                    Pg.dma("sp", x1_s[t * 128:t * 128 + rows, :], M[:rows, gi, :], [Mb.sub(2 * gi, 2 * gi + 2)], [x1b], ysem[gi])
                    Pg.dma("sp", hfT_s[t].rearrange("p (k t) -> p k t", t=128)[:, :, :rows], R1f[:, :, gi * 128:gi * 128 + rows], [R1b], [x1b], R1sem[gi])
                Pg.flush()
                gs2.close()
        GF = 5
        fgroups = [list(range(i, min(NOWN, i + GF))) for i in range(0, NOWN, GF)]
        with ExitStack() as ph:
            R3 = sb(ph, "p3_R3", [128, KC, GF * 128], BF16)
            R1b = Buf()
            R3sem = [Pg.dsem(f"p3R3_{i}") for i in range(GF)]
            M = sb(ph, "p3_Y", [128, GF, D], F32)
            Mb = Buf(2 * GF)
            ysem = [Pg.dsem(f"p3y{i}") for i in range(GF)]
            NWB = 4
            WB = [sb(ph, f"p3_WB{i}", [128, WBE], BF16) for i in range(NWB)]
            WBb = [Buf() for _ in range(NWB)]
            WBsem = [Pg.dsem(f"p3WB{i}") for i in range(NWB)]
            wpend = [False] * NWB
            actT = [sb(ph, f"p3_actT{i}", [128, NFC, GF * 128], BF16) for i in range(2)]
            actTb = [Buf() for _ in range(2)]
            rl = [sb(ph, f"p3_rl{i}", [128, 512], F32) for i in range(2)]
            rlb = [Buf() for _ in range(2)]
            st2 = {"wb": 0, "fb": 0, "tb": 0, "bc": 0, "e": 0, "oa": 0, "sg": 0}
            for gidx, tiles in enumerate(fgroups):
                ng = len(tiles)
                rws = [cfg.rows(t) for t in tiles]
                ntok = ng * 128
                R1f = R3[:, :, 0:ntok]
                for gi, t in enumerate(tiles):
                    rows = rws[gi]
                    Pg.dma("sp", M[:rows, gi, :], x1_s[t * 128:t * 128 + rows, :], [x1b], [Mb.sub(2 * gi, 2 * gi + 2)], ysem[gi])
                    Pg.dma("sp", R3[:, :, gi * 128:gi * 128 + rows], hfT_s[t].rearrange("p (k t) -> p k t", t=128)[:, :, :rows], [x1b], [R1b], R3sem[gi])

                ntk = (ng - 1) * 128 + rws[-1]
                mv_tiles = [(m0, min(512, ntk - m0)) for m0 in range(0, ntk, 512)]
                NFB = F // FBLK
                UPC = min(CB, FBLK)
                NUP = FBLK // UPC
                DNC = min(2, NFC)
                NDN = NFC // DNC

                def load_ups(fbk):
                    return [load_w(wu3, fbk * FBLK + u * UPC, UPC) for u in range(NUP)]

                def load_dns(fbk):
                    dns = []
                    for d_ in range(NDN):
                        i = st2["wb"] % NWB
                        st2["wb"] += 1
                        assert not wpend[i], "weight slot overwritten before its consumers were generated"
                        wpend[i] = True
                        c0 = fbk * NFC + d_ * DNC
                        Pg.op("pool", lambda e, i=i, c0=c0: [e.dma_start(out=WB[i][:, 0:DNC * D].rearrange("p (k c) -> p k c", c=D), in_=wd3[:, c0:c0 + DNC, :])],
                              [], [WBb[i]], sem=WBsem[i])
                        dns.append(i)
                    return dns

                nups = load_ups(0)
                ndns = load_dns(0)
                for fbk in range(NFB):
                    ups, dns = nups, ndns
                    ai = fbk % 2
                    for fc in range(NFC):
                        u = (fc * 128) // UPC
                        uo = (fc * 128) % UPC
                        for (m0, mn) in mv_tiles:
                            fb = st2["fb"] % 2
                            st2["fb"] += 1
                            Pg.mm_group([(FB_[fb][:, 0:mn], wv(ups[u], UPC)[:, kc, uo:uo + 128], R1f[:, kc, m0:m0 + mn], kc == 0, kc == KC - 1) for kc in range(KC)],
                                        [R1b, WBb[ups[u]]], [FBb[fb]])
                            ri = st2["e"] % 2
                            st2["e"] += 1
                            Pg.op("act", lambda e, fb=fb, ri=ri, mn=mn: e.activation(out=rl[ri][:, 0:mn], in_=FB_[fb][:, 0:mn], func=AF.Relu),
                                  [FBb[fb]], [rlb[ri]])
                            Pg.op("pool", lambda e, ri=ri, ai=ai, fc=fc, m0=m0, mn=mn: e.tensor_tensor(out=actT[ai][:, fc, m0:m0 + mn], in0=rl[ri][:, 0:mn], in1=rl[ri][:, 0:mn], op=ALU.mult),
                                  [rlb[ri]], [actTb[ai]])
                    rel(*ups)
                    if fbk + 1 < NFB:
                        nups = load_ups(fbk + 1)
                    for gi, t in enumerate(tiles):
                        rows = rws[gi]
                        for db in range(0, D, 512):
                            dn = min(512, D - db)
                            fb = 2 + st2["tb"] % 4
                            st2["tb"] += 1
                            mms = []
                            for fc in range(NFC):
                                mms.append((FB_[fb][:rows, 0:dn], actT[ai][:, fc, gi * 128:gi * 128 + rows],
                                            WB[dns[fc // DNC]][:, 0:DNC * D].rearrange("p (k c) -> p k c", c=D)[:, fc % DNC, db:db + dn], fc == 0, fc == NFC - 1))
                            Pg.mm_group(mms, [actTb[ai]] + [WBb[i] for i in dns], [FBb[fb]])
                            Pg.op("dve", lambda e, fb=fb, gi=gi, db=db, dn=dn, rows=rows: e.tensor_tensor(
                                out=M[:rows, gi, db:db + dn], in0=FB_[fb][:rows, 0:dn], in1=M[:rows, gi, db:db + dn], op=ALU.add),
                                [FBb[fb]], [Mb.sub(2 * gi, 2 * gi + 2)])
                    rel(*dns)
                    if fbk + 1 < NFB:
                        ndns = load_dns(fbk + 1)
                for gi, t in enumerate(tiles):
                    rows = rws[gi]
                    Pg.dma("sp", y_own[t * 128:t * 128 + rows, :], M[:rows, gi, :], [Mb.sub(2 * gi, 2 * gi + 2)], [], ysem[gi])
            Pg.flush()

            Pg.flush()
    return nc


def rope_tables(pos):
    half = 64
    inv = (ROPE_THETA ** (-np.arange(half, dtype=np.float32) / half)).astype(np.float32)
    ang = pos.astype(np.float32)[:, None] * inv[None, :]
    cos = np.cos(ang).astype(np.float32)
    sin = np.sin(ang).astype(np.float32)
    return np.concatenate([cos, cos, -sin, sin], axis=1).astype(np.float32)


def make_in_maps(cfg, inp):
    D, H, T, P, G = cfg.D, cfg.H, cfg.T, cfg.P, cfg.G
    NJ, SROWS, OWNROWS, KC = cfg.NJ, cfg.SROWS, cfg.OWNROWS, cfg.KC
    f = np.float32
    a = {k: np.asarray(v) for k, v in inp.items()}
    HWt = H * 256
    ident = np.eye(128, dtype=f)
    sidx = np.arange(128)
    tril = (sidx[:, None] <= sidx[None, :]).astype(f)
    trils = np.zeros((64, 64), f)
    for b0 in (0, 32):
        trils[b0:b0 + 16, b0:b0 + 16] = tril[:16, :16]
    w_s = a["w_s"][0]
    wsT = np.ascontiguousarray(np.transpose(w_s, (2, 0, 1))).reshape(128, G * 128)
    wsS = np.zeros((64, G, 64), f)
    for b0 in (0, 32):
        wsS[b0:b0 + 16, :, b0:b0 + 16] = np.transpose(w_s[:, :16, :16], (2, 0, 1))
    wsS = wsS.reshape(64, G * 64)
    bs_t = np.ascontiguousarray(a["b_s"][0].T)
    bs_s = np.zeros((64, G), f)
    for b0 in (0, 32):
        bs_s[b0:b0 + 16] = bs_t[:16]
    bc = lambda v: np.ascontiguousarray(np.broadcast_to(np.asarray(v, f)[None, :], (128, v.shape[0])))
    lam4 = np.concatenate([bc(a["lambda_q1"][0]), bc(a["lambda_k1"][0]), bc(a["lambda_q2"][0]), bc(a["lambda_k2"][0])], axis=1)
    common = {
        "ident": ident, "tril": tril, "trils": trils,
        "w_in": np.ascontiguousarray(a["w_in"][0]), "bgate_bc": bc(a["b_gate"][0]),
        "qn_bc": bc(a["q_norm"][0]), "kn_bc": bc(a["k_norm"][0]), "lam4": lam4,
        "subln_bc": bc(a["subln"][0]), "sgu_bc": bc(a["sgu_norm"][0]),
        "wsT": wsT, "wsS": wsS, "bs_t": bs_t, "bs_s": bs_s,
        "w_o": np.ascontiguousarray(a["w_o"][0]),
        "nmix_pk": np.ascontiguousarray(a["norm_mix"][0].reshape(KC, 128).T),
        "nffn_pk": np.ascontiguousarray(a["norm_ffn"][0].reshape(KC, 128).T),
        "w_up": np.ascontiguousarray(a["w_up"][0]), "w_down": np.ascontiguousarray(a["w_down"][0]),
        "cs_all": rope_tables(np.arange(T)),
    }
    spos = P + np.arange(cfg.ST)
    in_maps = []
    for c in range(N_CORES):
        b, r = c // 4, c % 4
        m = dict(common)
        m["x_all"] = np.ascontiguousarray(a["x_prompt"][b])
        xo = np.zeros((OWNROWS, D), f)
        pos = np.zeros((OWNROWS,), np.int64)
        for j in range(NJ):
            t = 4 * j + r
            xo[j * 128:(j + 1) * 128] = a["x_prompt"][b, t * 128:(t + 1) * 128]
            pos[j * 128:(j + 1) * 128] = np.arange(t * 128, (t + 1) * 128)
        for s in range(2):
            xo[NJ * 128 + 32 * s: NJ * 128 + 32 * s + 16] = a["x_sample"][2 * c + s]
            pos[NJ * 128 + 32 * s: NJ * 128 + 32 * s + 16] = spos
        m["x_own"] = xo
        m["cs_own"] = rope_tables(pos)
        m["cache_k"] = np.ascontiguousarray(a["cache_k_attn"][0, 2 * c:2 * c + 2].reshape(2, P, HWt))
        m["cache_v"] = np.ascontiguousarray(a["cache_v_attn"][0, 2 * c:2 * c + 2].reshape(2, P, HWt))
        mk = np.zeros((4, 128, 128), f)
        kk = np.arange(128)
        for mt in range(4):
            if mt < r:
                mk[mt] = 1.0
            elif mt == r:
                mk[mt] = ((kk[:, None] // 64) <= (kk[None, :] // 64)).astype(f)
        m["masks"] = np.ascontiguousarray(np.transpose(mk, (1, 0, 2))).reshape(128, 4 * 128)
        in_maps.append(m)
    return in_maps


def assemble(cfg, res):
    D, H, T = cfg.D, cfg.H, cfg.T
    NJ, SB, ST, NB = cfg.NJ, cfg.SB, cfg.ST, cfg.NB
    f = np.float32
    y_p = np.zeros((NB, T, D), f)
    y_s = np.zeros((SB, ST, D), f)
    k_p = np.zeros((1, NB, T, H, 256), f)
    v_p = np.zeros((1, NB, T, H, 256), f)
    k_s = np.zeros((1, SB, ST, H, 256), f)
    v_s = np.zeros((1, SB, ST, H, 256), f)
    sg = np.zeros((1, SB, ST, D), f)
    for c in range(N_CORES):
        b, r = c // 4, c % 4
        o = res[c]
        yo = np.asarray(o["y_own"])
        for j in range(NJ):
            t = 4 * j + r
            y_p[b, t * 128:(t + 1) * 128] = yo[j * 128:(j + 1) * 128]
        for s in range(2):
            r0 = NJ * 128 + 32 * s
            y_s[2 * c + s] = yo[r0:r0 + 16]
            k_s[0, 2 * c + s] = np.asarray(o["k_s"])[32 * s:32 * s + 16].reshape(ST, H, 256)
            v_s[0, 2 * c + s] = np.asarray(o["v_s"])[32 * s:32 * s + 16].reshape(ST, H, 256)
            sg[0, 2 * c + s] = np.asarray(o["sgu_s"])[32 * s:32 * s + 16]
        if r == 0:
            k_p[0, b] = np.asarray(o["k_all"]).reshape(T, H, 256)
            v_p[0, b] = np.asarray(o["v_all"]).reshape(T, H, 256)
    return (y_p, y_s, k_p, v_p, k_s, v_s, sg)


def run(cfg, inputs, stop=9):
    nc = build_program(cfg, stop)
    in_maps = make_in_maps(cfg, inputs)
    res = run_bass_kernel_spmd(nc, in_maps, core_ids=list(range(N_CORES)))
    return assemble(cfg, res.results)


def kernel(**inputs):
    cfg = Cfg()
    return run(cfg, inputs)
```
